# Optimizing a Trainium2 kernel written in Bass

```python
import jax, jax.numpy as jnp
from jax import lax
import numpy as np

D_MODEL = 1024
BATCH = 4
SEQ = 4096
DEPTH = 1
DEC_BATCH = 4
DEC_SEQ = 8192
PAST_LEN = 128

HEAD_DIM = 128
N_HEADS_A = 8
N_HEADS_F = 8
D_A = N_HEADS_A * HEAD_DIM
D_F = N_HEADS_F * HEAD_DIM
D_MIX = D_A + D_F
D_IN = 4 * D_A + 2 * D_F
CONV_W = 3
ALPHA = float((2 * DEPTH) ** 0.25)
BETA = float((8 * DEPTH) ** -0.25)
LN_EPS = 1e-5

kernel_name = "hybrid_shortconv_fnet_encoder"


def _layernorm(r, g, b):
    rf = r.astype(jnp.float32)
    mu = jnp.mean(rf, axis=-1, keepdims=True)
    var = jnp.mean(jnp.square(rf - mu), axis=-1, keepdims=True)
    y = ((rf - mu) * lax.rsqrt(var + LN_EPS)).astype(r.dtype)
    return y * g + b


def _layer(x, c, w_ada, b_ada, w_in, conv_w, conv_b, w_fmix, w_out, ln_g, ln_b):
    bsz, seq, _ = x.shape
    mod = jax.nn.silu(c) @ w_ada + b_ada
    shift, scale, gate = jnp.split(mod, 3, axis=-1)
    h = x * (1.0 + scale[:, None, :]) + shift[:, None, :]

    z = h @ w_in
    bg, cg, va, ga, vf, gf = jnp.split(
        z, [D_A, 2 * D_A, 3 * D_A, 4 * D_A, 4 * D_A + D_F], axis=-1)

    u = cg * va
    up = jnp.pad(u, ((0, 0), (1, 1), (0, 0)))
    conv = (up[:, :-2] * conv_w[0] + up[:, 1:-1] * conv_w[1]
            + up[:, 2:] * conv_w[2] + conv_b)
    y_a = bg * conv * jax.nn.silu(ga)

    vg = vf.reshape(bsz, seq, N_HEADS_F, HEAD_DIM).astype(jnp.float32)
    fr = jnp.real(jnp.fft.fftn(vg, axes=(1, 3), norm="ortho")).astype(x.dtype)
    fr = jnp.einsum('bshd,hde->bshe', fr, w_fmix).reshape(bsz, seq, D_F)
    y_f = fr * jax.nn.silu(gf)

    o = jnp.concatenate([y_a, y_f], axis=-1) @ w_out
    return _layernorm(ALPHA * x + gate[:, None, :] * o, ln_g, ln_b)


def _trunk(x, c, w_ada, b_ada, w_in, conv_w, conv_b, w_fmix, w_out, ln_g, ln_b):
    for l in range(DEPTH):
        x = _layer(x, c, w_ada[l], b_ada[l], w_in[l], conv_w[l], conv_b[l],
                   w_fmix[l], w_out[l], ln_g[l], ln_b[l])
    return x


def setup_inputs(seed: int = 0) -> dict:
    key = jax.random.key(seed)
    ks = jax.random.split(key, 14)
    f32 = jnp.float32
    x_prompt = jax.random.normal(ks[0], (BATCH, SEQ, D_MODEL), f32)
    x_sample = jax.random.normal(ks[1], (DEC_BATCH, DEC_SEQ, D_MODEL), f32)
    c_prompt = jax.random.normal(ks[2], (BATCH, D_MODEL), f32)
    c_sample = jax.random.normal(ks[3], (DEC_BATCH, D_MODEL), f32)
    w_ada = jax.random.normal(ks[4], (DEPTH, D_MODEL, 3 * D_MODEL), f32) * (0.1 * D_MODEL ** -0.5)
    b_ada = jax.random.normal(ks[5], (DEPTH, 3 * D_MODEL), f32) * 0.02
    w_in = jax.random.normal(ks[6], (DEPTH, D_MODEL, D_IN), f32) * D_MODEL ** -0.5
    conv_w = jax.random.normal(ks[7], (DEPTH, CONV_W, D_A), f32) * CONV_W ** -0.5
    conv_b = jax.random.normal(ks[8], (DEPTH, D_A), f32) * 0.02
    w_fmix = jax.random.normal(ks[9], (DEPTH, N_HEADS_F, HEAD_DIM, HEAD_DIM), f32) * HEAD_DIM ** -0.5
    w_out = jax.random.normal(ks[10], (DEPTH, D_MIX, D_MODEL), f32) * (D_MIX ** -0.5 * BETA)
    ln_g = 1.0 + 0.02 * jax.random.normal(ks[11], (DEPTH, D_MODEL), f32)
    ln_b = 0.02 * jax.random.normal(ks[12], (DEPTH, D_MODEL), f32)
    return {"x_prompt": x_prompt, "x_sample": x_sample, "c_prompt": c_prompt, "c_sample": c_sample,
            "w_ada": w_ada, "b_ada": b_ada, "w_in": w_in, "conv_w": conv_w, "conv_b": conv_b,
            "w_fmix": w_fmix, "w_out": w_out, "ln_g": ln_g, "ln_b": ln_b}


def reference(x_prompt, x_sample, c_prompt, c_sample, w_ada, b_ada, w_in, conv_w, conv_b,
              w_fmix, w_out, ln_g, ln_b):
    y_prompt = _trunk(x_prompt, c_prompt, w_ada, b_ada, w_in, conv_w, conv_b, w_fmix, w_out, ln_g, ln_b)
    y_sample = _trunk(x_sample, c_sample, w_ada, b_ada, w_in, conv_w, conv_b, w_fmix, w_out, ln_g, ln_b)
    return (y_prompt, y_sample)
```

```python
import numpy as np
import ml_dtypes
from contextlib import ExitStack
import concourse.bass as bass
import concourse.mybir as mybir
from concourse.bass_utils import run_bass_kernel_spmd

F32 = mybir.dt.float32
BF16 = mybir.dt.bfloat16
AF = mybir.ActivationFunctionType
ALU = mybir.AluOpType

ALPHA = float(2.0 ** 0.25)
LN_EPS = 1e-5
SEQS = {"S": dict(S=8192, N2=64, q=0), "P": dict(S=4096, N2=32, q=1)}


class Ev:
    __slots__ = ("sem", "val")

    def __init__(self, sem, val):
        self.sem = sem
        self.val = val


class Prog:
    ENGS = ("sync", "scalar", "vector", "gpsimd", "tensor")

    def __init__(self, nc, stack):
        self.nc = nc
        self.stack = stack
        self.q = {e: [] for e in self.ENGS}
        self.sems = {}
        self.cnt = {}
        self.waited = {e: {} for e in self.ENGS}
        self.wr = {}
        self.rd = {}
        self.latest = {}

    def sem(self, name):
        if name not in self.sems:
            self.sems[name] = self.stack.enter_context(self.nc.semaphore(name))
            self.cnt[name] = 0
        return self.sems[name]

    def _wait(self, eng, sem, val):
        w = self.waited[eng]
        if w.get(sem, 0) >= val:
            return
        w[sem] = val
        s = self.sems[sem]
        self.q[eng].append(lambda e, s=s, v=val: e.wait_ge(s, v))

    def _deps(self, eng, reads, writes, extra):
        need = {}
        for k in reads:
            for s, v in self.wr.get(k, {}).items():
                need[s] = max(need.get(s, 0), v)
        for k in writes:
            for s, v in self.wr.get(k, {}).items():
                need[s] = max(need.get(s, 0), v)
            for s, v in self.rd.get(k, {}).items():
                need[s] = max(need.get(s, 0), v)
        for ev in extra:
            if ev is not None:
                need[ev.sem] = max(need.get(ev.sem, 0), ev.val)
        for s, v in need.items():
            self._wait(eng, s, v)

    def note(self, ev, reads=(), writes=()):
        for k in reads:
            d = self.rd.setdefault(k, {})
            d[ev.sem] = max(d.get(ev.sem, 0), ev.val)
        for k in writes:
            d = self.wr.setdefault(k, {})
            d[ev.sem] = max(d.get(ev.sem, 0), ev.val)
        self.latest[ev.sem] = max(self.latest.get(ev.sem, 0), ev.val)

    def op(self, eng, fn, reads=(), writes=(), extra=()):
        self._deps(eng, reads, writes, extra)
        name = "c_" + eng
        s = self.sem(name)
        self.cnt[name] += 1
        v = self.cnt[name]
        self.q[eng].append(lambda e, fn=fn, s=s: fn(e).then_inc(s, 1))
        ev = Ev(name, v)
        self.note(ev, reads, writes)
        return ev

    def op_steps(self, eng, steps, writes=()):
        self._deps(eng, (), writes, ())
        name = "c_" + eng
        s = self.sem(name)
        n = len(steps)
        all_reads = []
        for i, (fn, reads) in enumerate(steps):
            self._deps(eng, reads, (), ())
            all_reads.extend(reads)
            if i < n - 1:
                self.q[eng].append(lambda e, fn=fn: fn(e))
            else:
                self.cnt[name] += 1
                self.q[eng].append(lambda e, fn=fn, s=s: fn(e).then_inc(s, 1))
        ev = Ev(name, self.cnt[name])
        self.note(ev, all_reads, writes)
        return ev

    def dma(self, eng, out, in_, sem, reads=(), writes=(), extra=(), track=True):
        self._deps(eng, reads, writes, extra)
        s = self.sem(sem)
        self.cnt[sem] += 16
        v = self.cnt[sem]
        self.q[eng].append(lambda e, s=s, out=out, in_=in_: e.dma_start(out=out, in_=in_).then_inc(s, 16))
        ev = Ev(sem, v)
        if track:
            self.note(ev, reads, writes)
        else:
            self.latest[sem] = max(self.latest.get(sem, 0), v)
        return ev

    def barrier(self):
        for eng in self.ENGS:
            for s, v in self.latest.items():
                self._wait(eng, s, v)

    def emit(self):
        nc = self.nc
        qs = self.q
        with nc.Block() as block:
            @block.sync
            def _(e):
                for f in qs["sync"]:
                    f(e)

            @block.scalar
            def _(e):
                for f in qs["scalar"]:
                    f(e)

            @block.vector
            def _(e):
                for f in qs["vector"]:
                    f(e)

            @block.gpsimd
            def _(e):
                for f in qs["gpsimd"]:
                    f(e)

            @block.tensor
            def _(e):
                for f in qs["tensor"]:
                    f(e)
        self.q = {e: [] for e in self.ENGS}


def build_nc(dbg=False):
    nc = bass.Bass("TRN2", target_bir_lowering=False)

    def din(name, shape, dt=F32):
        return nc.dram_tensor(name, list(shape), dt, kind="ExternalInput")

    xTp = {"S": din("xTp_S", [1024, 8192]), "P": din("xTp_P", [1024, 4096])}
    xTo = {"S": din("xTo_S", [1024, 4096]), "P": din("xTo_P", [1024, 2048])}
    xo = {"S": din("xo_S", [4096, 1024]), "P": din("xo_P", [2048, 1024])}
    xTh_d = din("xTh", [1024, 32])
    hmask_d = din("hmask", [128, 8, 32])
    cT_d = din("cT", [128, 8, 2])
    wada_d = din("w_ada", [1024, 3072])
    bfm_d = din("b_fm", [128, 16, 2])
    bcast_d = din("bcast", [128, 3, 1024])
    win_d = din("w_in", [1024, 6144])
    wout_d = din("w_out", [2048, 1024])
    wfm_d = din("w_fmix", [8, 128, 128])
    convp_d = din("convp", [128, 8, 4])
    cdt_d = din("cdt", [128, 2, 128])
    MA_d = {"S": din("MA_S", [64, 128, 128], BF16), "P": din("MA_P", [32, 128, 128], BF16)}
    MB_d = {"S": din("MB_S", [128, 4, 64], BF16), "P": din("MB_P", [64, 4, 32], BF16)}
    y_d = {"S": nc.dram_tensor("y_S", [4096, 1024], F32, kind="ExternalOutput"),
           "P": nc.dram_tensor("y_P", [2048, 1024], F32, kind="ExternalOutput")}
    Gd = {"S": nc.dram_tensor("Gd_S", [2, 64, 64, 1024], BF16, kind="Internal"),
          "P": nc.dram_tensor("Gd_P", [2, 32, 64, 1024], BF16, kind="Internal")}
    FRd = {"S": nc.dram_tensor("FR_S", [8, 128, 4096], BF16, kind="Internal"),
           "P": nc.dram_tensor("FR_P", [8, 128, 2048], BF16, kind="Internal")}
    gate_d = nc.dram_tensor("gate_scr", [128, 2, 1024], F32, kind="Internal")

    with ExitStack() as G:
        _uid = [0]

        def sb(st, name, shape, dt):
            _uid[0] += 1
            return st.enter_context(nc.sbuf_tensor(f"sb{_uid[0]}_{name}", list(shape), dt))

        P = Prog(nc, G)
        ps = G.enter_context(nc.psum_tensor("ps", [128, 8, 512], F32))
        ss = sb(G, "ss", [128, 16, 2], F32)
        convp = sb(G, "convp", [128, 8, 4], F32)
        mb = {k: sb(G, "mb_" + k, list(MB_d[k].shape), BF16) for k in ("S", "P")}

        def mm(out, pairs, reads, writes):
            def fn(e):
                n = len(pairs)
                ins = None
                for i, (l, r) in enumerate(pairs):
                    ins = e.matmul(out, lhsT=l, rhs=r, start=(i == 0), stop=(i == n - 1))
                return ins
            return P.op("tensor", fn, reads=reads, writes=writes)

        def kcview(ap):
            return ap.rearrange("(k p) t -> p k t", p=128)

        with ExitStack() as B:
            wvf_b = sb(B, "wvf_b", [128, 8, 1024], BF16)
            with ExitStack() as C1:
                ma = {k: sb(C1, "ma_" + k, [128, SEQS[k]["N2"], 128], BF16) for k in ("P", "S")}
                xt = [sb(C1, f"xt{i}", [128, 8, 512], F32) for i in range(2)]
                ht = [sb(C1, f"ht{i}", [128, 8, 512], BF16) for i in range(2)]
                vsb = [sb(C1, f"vsb{i}", [128, 1024], BF16) for i in range(2)]
                gsb = [sb(C1, f"gsb{i}", [128, 4, 1024], BF16) for i in range(2)]
                ct = sb(C1, "ct", [128, 8, 2], F32)
                sct = sb(C1, "sct", [128, 8, 2], F32)
                sct_b = sb(C1, "sct_b", [128, 8, 2], BF16)
                ones = sb(C1, "ones", [128, 128], F32)
                rep = sb(C1, "rep", [128, 8, 2, 128], BF16)
                wada = [sb(C1, f"wada{i}", [128, 8, 512], BF16) for i in range(6)]
                bfm = sb(C1, "bfm", [128, 16, 2], F32)
                bc0 = sb(C1, "bc0", [128, 1024], F32)
                gate_sb = sb(C1, "gate_sb", [128, 2, 1024], F32)

                P.dma("sync", ct[:], cT_d.ap(), "d_A", track=False)
                P.dma("sync", bfm[:], bfm_d.ap(), "d_A", track=False)
                P.dma("sync", bc0[:], bcast_d.ap()[:, 0, :], "d_A", track=False)
                P.dma("sync", convp[:], convp_d.ap(), "d_A", track=False)
                P.dma("sync", mb["S"][:], MB_d["S"].ap(), "d_A", track=False)
                evA = P.dma("sync", mb["P"][:], MB_d["P"].ap(), "d_A", track=False)
                P.note(evA, writes=["ct", "bfm", "bc0", "convp", "mb"])
                for i in range(4):
                    P.dma("gpsimd", wada[i][:], kcview(wada_d.ap()[:, i * 512:(i + 1) * 512]), f"d_wa{i}", writes=[("wada", i)])
                for i in range(2):
                    P.dma("gpsimd", wvf_b[:, :, i * 512:(i + 1) * 512], kcview(win_d.ap()[:, 4096 + i * 512:4096 + (i + 1) * 512]),
                          f"d_wvf{i}", writes=[("wvf", i)])
                for sk in ("P", "S"):
                    for i in range(SEQS[sk]["N2"] // 16):
                        P.dma("gpsimd", ma[sk][:, i * 16:(i + 1) * 16, :],
                              MA_d[sk].ap()[i * 16:(i + 1) * 16].rearrange("a p c -> p a c"),
                              f"d_ma{sk}{i}", writes=[("ma", sk, i)])
                for i in range(4, 6):
                    P.dma("gpsimd", wada[i][:], kcview(wada_d.ap()[:, i * 512:(i + 1) * 512]), f"d_wa{i}", writes=[("wada", i)])

                P.op("scalar", lambda e: e.activation(out=sct[:], in_=ct[:], func=AF.Silu), reads=["ct"], writes=["sct"])
                P.op("scalar", lambda e: e.copy(out=sct_b[:], in_=sct[:]), reads=["sct"], writes=["sct_b"])
                P.op("vector", lambda e: e.memset(ones[:], 1.0), writes=["ones"])
                for kc in range(8):
                    for q in range(2):
                        P.op("vector", lambda e, kc=kc, q=q: e.tensor_scalar(
                            out=rep[:, kc, q, :], in0=ones[:], scalar1=sct[:, kc, q:q + 1], scalar2=None, op0=ALU.mult),
                            reads=["ones", "sct"], writes=[("rep", kc, q)])
                for ch in range(16):
                    piece = wada[ch // 4]
                    col = (ch % 4) * 128
                    mm(ps[:, 0, 2 * ch:2 * ch + 2], [(piece[:, kc, col:col + 128], sct_b[:, kc, :]) for kc in range(8)],
                       reads=[("wada", ch // 4), "sct_b"], writes=[("ps", 0)])
                P.op("vector", lambda e: e.tensor_tensor(out=ss[:], in0=ps[:, 0, 0:32].rearrange("p (c q) -> p c q", q=2),
                                                         in1=bfm[:], op=ALU.add),
                     reads=[("ps", 0), "bfm"], writes=["ss"])
                P.op("vector", lambda e: e.tensor_scalar_add(out=ss[:, 8:16, :], in0=ss[:, 8:16, :], scalar1=1.0),
                     reads=["ss"], writes=["ss"])
                batches = [(sk, b) for sk in ("P", "S") for b in range(SEQS[sk]["N2"] // 4)]

                def load_x(bi):
                    sk, b = batches[bi]
                    xb = bi % 2
                    P.dma("sync", xt[xb][:], kcview(xTp[sk].ap()[:, b * 512:(b + 1) * 512]), f"d_xt{xb}",
                          writes=[("xt", xb)])

                def hconv(bi):
                    sk, b = batches[bi]
                    q = SEQS[sk]["q"]
                    xb = bi % 2
                    for kc in range(8):
                        P.op("scalar", lambda e, xb=xb, kc=kc, q=q: e.activation(
                            out=ht[xb][:, kc, :], in_=xt[xb][:, kc, :], func=AF.Identity,
                            scale=ss[:, 8 + kc, q:q + 1], bias=ss[:, kc, q:q + 1]),
                            reads=[("xt", xb), "ss"], writes=[("ht", xb, kc)])

                def stage_a(bi, j):
                    sk, b_ = batches[bi]
                    s2 = 4 * b_ + j
                    vb = (4 * bi + j) % 2
                    gslot = bi % 2
                    par = (4 * bi + j) % 2
                    for half in range(2):
                        bank = 4 + 2 * par + half
                        mm(ps[:, bank, :], [(ma[sk][:, s2, :], vsb[vb][:, half * 512:(half + 1) * 512])],
                           reads=[("ma", sk, s2 // 16), ("vsb", vb, half)], writes=[("ps", bank)])
                        if half == 0:
                            P.op("scalar", lambda e, gslot=gslot, j=j, half=half, bank=bank: e.copy(
                                out=gsb[gslot][:, j, half * 512:(half + 1) * 512], in_=ps[:, bank, :]),
                                reads=[("ps", bank)], writes=[("gsb", gslot, j, half)])
                        else:
                            P.op("vector", lambda e, gslot=gslot, j=j, half=half, bank=bank: e.tensor_copy(
                                out=gsb[gslot][:, j, half * 512:(half + 1) * 512], in_=ps[:, bank, :]),
                                reads=[("ps", bank)], writes=[("gsb", gslot, j, half)])
                    if j == 3:
                        for ri in range(2):
                            P.dma("gpsimd", Gd[sk].ap()[ri, 4 * b_:4 * b_ + 4].rearrange("a k f -> k a f"),
                                  gsb[gslot][ri * 64:(ri + 1) * 64, :, :], f"d_g{gslot}{ri}",
                                  reads=[("gsb", gslot, jj, h) for jj in range(4) for h in range(2)])

                nbt = len(batches)
                load_x(0)
                load_x(1)
                hconv(0)
                prev = None
                for bi in range(nbt):
                    xb = bi % 2
                    if bi + 1 < nbt:
                        hconv(bi + 1)
                    if bi + 2 < nbt:
                        load_x(bi + 2)
                    for j in range(4):
                        vb = (4 * bi + j) % 2
                        for half in range(2):
                            mm(ps[:, vb * 2 + half, :],
                               [(ht[xb][:, kc, j * 128:(j + 1) * 128], wvf_b[:, kc, half * 512:(half + 1) * 512]) for kc in range(8)],
                               reads=[("ht", xb, kc) for kc in range(8)] + [("wvf", half)], writes=[("ps", vb * 2 + half)])
                            P.op("vector", lambda e, vb=vb, half=half: e.tensor_copy(
                                out=vsb[vb][:, half * 512:(half + 1) * 512], in_=ps[:, vb * 2 + half, :]),
                                reads=[("ps", vb * 2 + half)], writes=[("vsb", vb, half)])
                        if prev is not None:
                            stage_a(*prev)
                        prev = (bi, j)
                stage_a(*prev)
                for q in range(2):
                    for half in range(2):
                        bank = 4 + q * 2 + half
                        mm(ps[:, bank, :], [(rep[:, kc, q, :], wada[4 + half][:, kc, :]) for kc in range(8)],
                           reads=[("rep", kc, q) for kc in range(8)] + [("wada", 4 + half)], writes=[("ps", bank)])
                        P.op("vector", lambda e, q=q, half=half, bank=bank: e.tensor_tensor(
                            out=gate_sb[:, q, half * 512:(half + 1) * 512], in0=ps[:, bank, :],
                            in1=bc0[:, half * 512:(half + 1) * 512], op=ALU.add),
                            reads=[("ps", bank), "bc0"], writes=[("gate_sb", q, half)])
                P.dma("gpsimd", gate_d.ap(), gate_sb[:], "d_gs",
                      reads=[("gate_sb", q, h) for q in range(2) for h in range(2)], writes=["gate_d"])
                P.barrier()
                P.emit()

        with ExitStack() as W:
            win_b = sb(W, "win_b", [128, 8, 5120], BF16)
            F2S = ExitStack()
            cw_b = sb(F2S, "cw_b", [128, 2, 8, 128], BF16)
            with ExitStack() as C2:
                gt = [sb(C2, f"gt{i}", [128, 32, 256], BF16) for i in range(3)]
                psb = [sb(C2, f"psb{i}", [128, 2, 32, 128], BF16) for i in range(2)]
                frs = [sb(C2, f"frs{i}", [128, 4096], BF16) for i in range(2)]
                cdt = sb(C2, "cdt", [128, 2, 128], F32)
                wfm = sb(C2, "wfm", [128, 1024], F32)
                P.dma("sync", cdt[:], cdt_d.ap(), "d_C", track=False)
                evC = P.dma("sync", wfm[:].rearrange("p (g e) -> p g e", g=8), wfm_d.ap().rearrange("g m e -> m g e"), "d_C", track=False)
                P.note(evC, writes=["cdt", "wfm"])
                for cs in range(2):
                    for gh in range(2):
                        bank = 4 + cs * 2 + gh
                        mm(ps[:, bank, :], [(cdt[:, cs, :], wfm[:, gh * 512:(gh + 1) * 512])],
                           reads=["cdt", "wfm"], writes=[("ps", bank)])
                        P.op("scalar", lambda e, cs=cs, gh=gh, bank=bank: e.copy(
                            out=cw_b[:, cs, gh * 4:(gh + 1) * 4, :], in_=ps[:, bank, :].rearrange("p (g e) -> p g e", g=4)),
                            reads=[("ps", bank)], writes=["cw_b"])
                for i in range(10):
                    src = i * 512 if i < 8 else 5120 + (i - 8) * 512
                    P.dma("gpsimd", win_b[:, :, i * 512:(i + 1) * 512], kcview(win_d.ap()[:, src:src + 512]), f"d_wr{i % 4}", writes=["wres"])
                gcount = 0
                gtc = 0
                bankc = 0
                cc = 0
                ev_i = 0
                for sk in ("P", "S"):
                    N2 = SEQS[sk]["N2"]
                    K2 = 2 * N2
                    T = 64 * N2
                    H2 = N2 // 2
                    nslot = 512 // N2
                    Gv = Gd[sk].ap().rearrange("r a k f -> (r a) k f")
                    psb_all = {pb: [("psb", pb, kk) for kk in range(128)] for pb in range(2)}
                    for gp in range(4):
                        pbs = [0, 1]
                        gcount += 1
                        for kb in range(2):
                            gi = gtc % 3
                            gtc += 1
                            P.dma("sync", gt[gi][0:K2], Gv[:, kb * 32:(kb + 1) * 32, gp * 256:(gp + 1) * 256], f"d_gt{gi}",
                                  writes=[("gt", gi)])
                            runs = [[(k, k - 32 * kb, 2 if k == 0 else 0) for k in range(32 * kb, 32 * kb + 32)]]
                            if kb == 0:
                                runs.append([(64, 0, 3)])
                                runs.append([(k, 128 - k, 1) for k in range(97, 128)])
                            else:
                                runs.append([(k, 128 - k - 32, 1) for k in range(65, 97)])
                            chunks = []
                            for run in runs:
                                for c0 in range(0, len(run), 2 * nslot):
                                    chunks.append(run[c0:c0 + 2 * nslot])
                            for gg in range(2):
                                pb = pbs[gg]
                                for chunk in chunks:
                                    bset = (bankc % 3) * 2
                                    bankc += 1

                                    def fnb(e, gi=gi, chunk=chunk, bset=bset, N2=N2, K2=K2, nslot=nslot, sk=sk, gg=gg):
                                        ins = None
                                        for idx, (k1o, loc, mat) in enumerate(chunk):
                                            bb, sl = divmod(idx, nslot)
                                            ins = e.matmul(ps[:, bset + bb, sl * N2:(sl + 1) * N2],
                                                           lhsT=gt[gi][0:K2, loc, gg * 128:(gg + 1) * 128],
                                                           rhs=mb[sk][0:K2, mat, :], start=True, stop=True)
                                        return ins
                                    P.op("tensor", fnb, reads=[("gt", gi), "mb"], writes=[("ps", bset), ("ps", bset + 1)])
                                    n = len(chunk)
                                    k1_0 = chunk[0][0]
                                    keys = [("psb", pb, kk) for kk in range(k1_0, k1_0 + n)]
                                    oap = psb[pb][:, :, 0:H2, k1_0:k1_0 + n]
                                    iap = ps[:, bset:bset + 2, :].rearrange("p b (s r k) -> p r k (b s)", s=nslot, r=2)[:, :, :, 0:n]
                                    if ev_i % 2 == 0:
                                        P.op("scalar", lambda e, oap=oap, iap=iap: e.copy(out=oap, in_=iap),
                                             reads=[("ps", bset), ("ps", bset + 1)], writes=keys)
                                    else:
                                        P.op("vector", lambda e, oap=oap, iap=iap: e.tensor_copy(out=oap, in_=iap),
                                             reads=[("ps", bset), ("ps", bset + 1)], writes=keys)
                                    ev_i += 1
                        for gg in range(2):
                            g = gp * 2 + gg
                            pb = pbs[gg]
                            fb = pb
                            frkeys = []
                            for j in range(T // 512):
                                bank = 6 + cc % 2
                                cc += 1
                                mm(ps[:, bank, :],
                                   [(cw_b[:, cs, g, :], psb[pb][:, cs, 4 * j:4 * j + 4, :].rearrange("p a b -> p (a b)")) for cs in range(2)],
                                   reads=psb_all[pb] + ["cw_b"], writes=[("ps", bank)])
                                key = ("frs", fb, j)
                                frkeys.append(key)
                                if ev_i % 2 == 0:
                                    P.op("scalar", lambda e, fb=fb, j=j, bank=bank: e.copy(out=frs[fb][:, j * 512:(j + 1) * 512], in_=ps[:, bank, :]),
                                         reads=[("ps", bank)], writes=[key])
                                else:
                                    P.op("vector", lambda e, fb=fb, j=j, bank=bank: e.tensor_copy(out=frs[fb][:, j * 512:(j + 1) * 512], in_=ps[:, bank, :]),
                                         reads=[("ps", bank)], writes=[key])
                                ev_i += 1
                            P.dma("scalar", FRd[sk].ap()[g], frs[fb][:, 0:T], f"d_fr{fb}", reads=frkeys)
                P.barrier()
                P.emit()
            F2S.close()
            with ExitStack() as M:
                wout_b = sb(M, "wout_b", [128, 16, 1024], BF16)
                for kh in range(2):
                    for cq in range(2):
                        P.dma("gpsimd", wout_b[:, kh * 8:(kh + 1) * 8, cq * 512:(cq + 1) * 512],
                              kcview(wout_d.ap()[kh * 1024:(kh + 1) * 1024, cq * 512:(cq + 1) * 512]), f"d_wo{kh * 2 + cq}", writes=["wout"])
                xt = sb(M, "m_xt", [128, 8, 512], F32)
                ht = sb(M, "m_ht", [128, 8, 512], BF16)
                yt = sb(M, "m_yt", [128, 16, 512], BF16)
                fr = sb(M, "m_fr", [128, 2, 512], BF16)
                xtok = sb(M, "m_xtok", [128, 1, 1024], F32)
                rbuf = sb(M, "m_rbuf", [128, 2, 1024], F32)
                ybuf = sb(M, "m_ybuf", [128, 2, 1024], F32)
                bc = sb(M, "m_bc", [128, 3, 1024], F32)
                cgsb = sb(M, "m_cgsb", [128, 2, 512], F32)
                sg = sb(M, "m_sg", [128, 2, 512], F32)
                uext = sb(M, "m_uext", [128, 514], F32)
                acc = sb(M, "m_acc", [128, 512], F32)
                y1 = sb(M, "m_y1", [128, 1, 512], F32)
                xth = sb(M, "m_xth", [128, 8, 32], F32)
                hth = sb(M, "m_hth", [128, 8, 32], BF16)
                cgh = sb(M, "m_cgh", [128, 8, 32], F32)
                uh = sb(M, "m_uh", [128, 8, 32], F32)
                hmask = sb(M, "m_hmask", [128, 8, 32], F32)
                stats = sb(M, "m_stats", [128, 12], F32)
                mv = sb(M, "m_mv", [128, 2], F32)
                rstd = sb(M, "m_rstd", [128, 1], F32)
                epst = sb(M, "m_epst", [128, 1], F32)
                nmr = sb(M, "m_nmr", [128, 1], F32)
                P.op("vector", lambda e: e.memset(epst[:], LN_EPS), writes=["epst"])

                P.dma("sync", xth[:], kcview(xTh_d.ap()), "d_M", track=False)
                P.dma("sync", hmask[:], hmask_d.ap(), "d_M", track=False)
                evM = P.dma("sync", bc[:, 1:3, :], bcast_d.ap()[:, 1:3, :], "d_M", track=False)
                P.note(evM, writes=["xth", "hmask", "lngb"])
                for kc in range(8):
                    for q, (c0, c1) in ((0, (0, 16)), (1, (16, 32))):
                        P.op("scalar", lambda e, kc=kc, q=q, c0=c0, c1=c1: e.activation(
                            out=hth[:, kc, c0:c1], in_=xth[:, kc, c0:c1], func=AF.Identity,
                            scale=ss[:, 8 + kc, q:q + 1], bias=ss[:, kc, q:q + 1]),
                            reads=["xth", "ss"], writes=[("hth", kc, q)])
                hth_keys = [("hth", kc, q) for kc in range(8) for q in range(2)]
                for c in range(8):
                    mm(ps[:, 0, c * 32:(c + 1) * 32], [(win_b[:, kc, 1024 + c * 128:1024 + (c + 1) * 128], hth[:, kc, :]) for kc in range(8)],
                       reads=hth_keys + ["wres"], writes=[("ps", 0)])
                    mm(ps[:, 1, c * 32:(c + 1) * 32], [(win_b[:, kc, 2048 + c * 128:2048 + (c + 1) * 128], hth[:, kc, :]) for kc in range(8)],
                       reads=hth_keys + ["wres"], writes=[("ps", 1)])
                P.op("scalar", lambda e: e.copy(out=cgh[:], in_=ps[:, 0, 0:256].rearrange("p (c t) -> p c t", c=8)),
                     reads=[("ps", 0)], writes=["cgh"])
                P.op("vector", lambda e: e.tensor_tensor(out=uh[:], in0=cgh[:], in1=ps[:, 1, 0:256].rearrange("p (c t) -> p c t", c=8), op=ALU.mult),
                     reads=["cgh", ("ps", 1)], writes=["uh"])
                P.op("vector", lambda e: e.tensor_tensor(out=uh[:], in0=uh[:], in1=hmask[:], op=ALU.mult),
                     reads=["uh", "hmask"], writes=["uh"])

                hbase = {"S": 0, "P": 16}
                tiles = [(sk, i) for sk in ("P", "S") for i in range(SEQS[sk]["S"] // 2 // 512)]

                def load_xt(ti):
                    sk, i = tiles[ti]
                    P.dma("sync", xt[:], kcview(xTo[sk].ap()[:, i * 512:(i + 1) * 512]), "d_mxt", writes=["xt"])

                def hconv_m(ti):
                    sk, i = tiles[ti]
                    q = SEQS[sk]["q"]
                    for kc in range(8):
                        P.op("scalar", lambda e, kc=kc, q=q: e.activation(
                            out=ht[:, kc, :], in_=xt[:, kc, :], func=AF.Identity,
                            scale=ss[:, 8 + kc, q:q + 1], bias=ss[:, kc, q:q + 1]),
                            reads=["xt", "ss"], writes=[("ht", kc)])

                ht_keys = [("ht", kc) for kc in range(8)]
                yt_keys = [("yt", kc) for kc in range(16)]
                load_xt(0)
                hconv_m(0)
                for ti, (sk, i) in enumerate(tiles):
                    q = SEQS[sk]["q"]
                    if ti + 1 < len(tiles):
                        load_xt(ti + 1)
                    hl = hbase[sk] + 2 * i
                    for c in range(8):
                        st_ = (c % 2) * 4
                        tb = c % 2
                        for wi, off in enumerate((0, 1024, 2048, 3072)):
                            mm(ps[:, st_ + wi, :],
                               [(win_b[:, kc, off + c * 128:off + (c + 1) * 128], ht[:, kc, :]) for kc in range(8)],
                               reads=ht_keys + ["wres"], writes=[("ps", st_ + wi)])
                        P.op("scalar", lambda e, tb=tb, st_=st_: e.copy(out=cgsb[:, tb, :], in_=ps[:, st_ + 1, :]),
                             reads=[("ps", st_ + 1)], writes=[("cgsb", tb)])
                        P.op("scalar", lambda e, tb=tb, st_=st_: e.activation(out=sg[:, tb, :], in_=ps[:, st_ + 3, :], func=AF.Silu),
                             reads=[("ps", st_ + 3)], writes=[("sg", tb)])
                        P.op("gpsimd", lambda e, c=c, hl=hl: e.tensor_copy(out=uext[:, 0:1], in_=uh[:, c, hl:hl + 1]),
                             reads=["uh"], writes=["uextL"])
                        P.op("gpsimd", lambda e, c=c, hl=hl: e.tensor_copy(out=uext[:, 513:514], in_=uh[:, c, hl + 1:hl + 2]),
                             reads=["uh"], writes=["uextR"])
                        P.op("vector", lambda e, tb=tb, st_=st_: e.tensor_tensor(out=uext[:, 1:513], in0=cgsb[:, tb, :], in1=ps[:, st_ + 2, :], op=ALU.mult),
                             reads=[("cgsb", tb), ("ps", st_ + 2)], writes=["uextM"])
                        P.op("vector", lambda e, c=c: e.tensor_scalar(out=acc[:], in0=uext[:, 1:513], scalar1=convp[:, c, 1:2], scalar2=convp[:, c, 3:4],
                                                                     op0=ALU.mult, op1=ALU.add),
                             reads=["uextM", "convp"], writes=["acc"])
                        P.op("vector", lambda e, c=c: e.scalar_tensor_tensor(out=acc[:], in0=uext[:, 0:512], scalar=convp[:, c, 0:1], in1=acc[:],
                                                                            op0=ALU.mult, op1=ALU.add),
                             reads=["uextM", "uextL", "acc"], writes=["acc"])
                        P.op("vector", lambda e, c=c: e.scalar_tensor_tensor(out=acc[:], in0=uext[:, 2:514], scalar=convp[:, c, 2:3], in1=acc[:],
                                                                            op0=ALU.mult, op1=ALU.add),
                             reads=["uextM", "uextR", "acc"], writes=["acc"])
                        P.op("vector", lambda e, tb=tb, st_=st_: e.tensor_tensor(out=y1[:, 0, :], in0=acc[:], in1=ps[:, st_, :], op=ALU.mult),
                             reads=["acc", ("ps", st_)], writes=["y1"])
                        P.op("gpsimd", lambda e, tb=tb, c=c: e.tensor_tensor(out=yt[:, c, :], in0=y1[:, 0, :], in1=sg[:, tb, :], op=ALU.mult),
                             reads=["y1", ("sg", tb)], writes=[("yt", c)])
                    for c in range(8):
                        bank = c
                        fb = c % 2
                        mm(ps[:, bank, :],
                           [(win_b[:, kc, 4096 + c * 128:4096 + (c + 1) * 128], ht[:, kc, :]) for kc in range(8)],
                           reads=ht_keys + ["wres"], writes=[("ps", bank)])
                        P.dma("sync", fr[:, fb, :], FRd[sk].ap()[c, :, i * 512:(i + 1) * 512], f"d_frl{fb}", writes=[("fr", fb)])
                        P.op("scalar", lambda e, fb=fb, bank=bank: e.activation(out=cgsb[:, fb, :], in_=ps[:, bank, :], func=AF.Silu),
                             reads=[("ps", bank)], writes=[("cgsb", fb)])
                        P.op("gpsimd", lambda e, fb=fb, c=c: e.tensor_tensor(out=yt[:, 8 + c, :], in0=cgsb[:, fb, :], in1=fr[:, fb, :], op=ALU.mult),
                             reads=[("cgsb", fb), ("fr", fb)], writes=[("yt", 8 + c)])
                    if ti + 1 < len(tiles):
                        hconv_m(ti + 1)
                    if i == 0:
                        P.dma("sync", bc[:, 0, :], gate_d.ap()[:, q, :], "d_gate", reads=["gate_d"], writes=["gate"])
                    def o_mm(sub):
                        ob = (sub % 2) * 4
                        for half in range(2):
                            steps = []
                            for kc in range(16):
                                steps.append((lambda e, kc=kc, sub=sub, half=half, ob=ob: e.matmul(
                                    ps[:, ob + half, :], lhsT=yt[:, kc, sub * 128:(sub + 1) * 128],
                                    rhs=wout_b[:, kc, half * 512:(half + 1) * 512], start=(kc == 0), stop=(kc == 15)),
                                    [("yt", kc), "wout"]))
                            P.op_steps("tensor", steps, writes=[("ps", ob + half)])

                    def o_t(sub):
                        ob = (sub % 2) * 4
                        rb = sub % 2
                        P.op("vector", lambda e, ob=ob, rb=rb: e.tensor_tensor(out=rbuf[:, rb, :].rearrange("p (a b) -> p a b", a=2), in0=ps[:, ob:ob + 2, :],
                                                                             in1=bc[:, 0, :].rearrange("p (a b) -> p a b", a=2), op=ALU.mult),
                             reads=[("ps", ob), ("ps", ob + 1), "gate"], writes=[("rbuf", rb)])

                    def o_chain(sub):
                        rb = sub % 2
                        r0 = i * 512 + sub * 128
                        P.dma("sync", xtok[:, 0, :], xo[sk].ap()[r0:r0 + 128, :], "d_xtok", writes=["xtok"])
                        P.op("scalar", lambda e: e.activation(out=xtok[:, 0, :], in_=xtok[:, 0, :], func=AF.Copy, scale=ALPHA),
                             reads=["xtok"], writes=["xtok"])
                        P.op("vector", lambda e, rb=rb: e.tensor_tensor(out=rbuf[:, rb, :], in0=rbuf[:, rb, :], in1=xtok[:, 0, :], op=ALU.add),
                             reads=["xtok", ("rbuf", rb)], writes=[("rbuf", rb)])
                        for hh in range(2):
                            P.op("vector", lambda e, hh=hh, rb=rb: e.bn_stats(out=stats[:, hh * 6:(hh + 1) * 6], in_=rbuf[:, rb, hh * 512:(hh + 1) * 512]),
                                 reads=[("rbuf", rb)], writes=[("stats", hh)])
                        P.op("vector", lambda e: e.bn_aggr(out=mv[:], in_=stats[:]), reads=[("stats", 0), ("stats", 1)], writes=["mv"])
                        P.op("scalar", lambda e: e.activation(out=rstd[:], in_=mv[:, 1:2], func=AF.Sqrt, bias=epst[:, 0:1], scale=1.0),
                             reads=["mv", "epst"], writes=["rstd"])
                        P.op("vector", lambda e: e.reciprocal(out=rstd[:], in_=rstd[:]),
                             reads=["rstd"], writes=["rstd"])
                        P.op("vector", lambda e: e.tensor_scalar(out=nmr[:], in0=mv[:, 0:1], scalar1=-1.0, scalar2=rstd[:, 0:1], op0=ALU.mult, op1=ALU.mult),
                             reads=["mv", "rstd"], writes=["nmr"])
                        P.op("scalar", lambda e, rb=rb: e.activation(out=ybuf[:, rb, :], in_=rbuf[:, rb, :], func=AF.Identity,
                                                                    scale=rstd[:, 0:1], bias=nmr[:, 0:1]),
                             reads=[("rbuf", rb), "rstd", "nmr"], writes=[("ybuf", rb)])
                        P.op("gpsimd", lambda e, rb=rb: e.tensor_tensor(out=ybuf[:, rb, :], in0=ybuf[:, rb, :], in1=bc[:, 1, :], op=ALU.mult),
                             reads=[("ybuf", rb), "lngb"], writes=[("ybuf", rb)])
                        P.op("gpsimd", lambda e, rb=rb: e.tensor_tensor(out=ybuf[:, rb, :], in0=ybuf[:, rb, :], in1=bc[:, 2, :], op=ALU.add),
                             reads=[("ybuf", rb), "lngb"], writes=[("ybuf", rb)])
                        P.dma("gpsimd", y_d[sk].ap()[r0:r0 + 128, :], ybuf[:, rb, :], f"d_y{rb}", reads=[("ybuf", rb)])

                    o_mm(0)
                    o_t(0)
                    o_mm(1)
                    o_t(1)
                    o_chain(0)
                    o_mm(2)
                    o_t(2)
                    o_chain(1)
                    o_mm(3)
                    o_t(3)
                    o_chain(2)
                    o_chain(3)
                P.barrier()
                P.emit()
    return nc


def _tables(S, r):
    N2 = S // 128
    s1 = np.arange(128)
    s2 = np.arange(N2)
    k1p = np.arange(64)
    s = (N2 * s1[None, :, None] + s2[:, None, None])
    ang = 2 * np.pi * ((k1p[None, None, :] * s) % S) / S
    re = np.cos(ang)
    im = -np.sin(ang)
    im[:, :, 0] = ((-1.0) ** s1)[None, :]
    MA = np.concatenate([re, im], axis=2) / np.sqrt(128.0)
    H2 = N2 // 2
    k2 = r * H2 + np.arange(H2)
    th = 2 * np.pi * ((k2[None, :] * s2[:, None]) % N2) / N2
    c, sn = np.cos(th), np.sin(th)
    Md = np.zeros((2, N2, 2, H2))
    Md[0, :, 0] = c
    Md[0, :, 1] = -sn
    Md[1, :, 0] = sn
    Md[1, :, 1] = c
    th1 = 2 * np.pi * (((k2[None, :] + 1) * s2[:, None]) % N2) / N2
    c1, s1_ = np.cos(th1), np.sin(th1)
    Mm = np.zeros((2, N2, 2, H2))
    Mm[0, :, 0] = c1
    Mm[0, :, 1] = -s1_
    Mm[1, :, 0] = -s1_
    Mm[1, :, 1] = -c1
    Md0 = Md.copy()
    Md0[1] = 0
    th64 = th + np.pi * s2[:, None] / N2
    M64 = np.zeros((2, N2, 2, H2))
    M64[1, :, 0] = np.cos(th64)
    M64[1, :, 1] = -np.sin(th64)
    MB4 = np.stack([Md, Mm, Md0, M64], axis=0).reshape(4, 2 * N2, N2) / np.sqrt(float(N2))
    return (np.ascontiguousarray(MA).astype(ml_dtypes.bfloat16),
            np.ascontiguousarray(MB4.transpose(1, 0, 2)).astype(ml_dtypes.bfloat16))


_NC_CACHE = {}


def kernel(x_prompt, x_sample, c_prompt, c_sample, w_ada, b_ada, w_in, conv_w, conv_b,
           w_fmix, w_out, ln_g, ln_b):
    f32 = np.float32
    x_prompt = np.asarray(x_prompt, f32)
    x_sample = np.asarray(x_sample, f32)
    if "nc" not in _NC_CACHE:
        _NC_CACHE["nc"] = build_nc()
    nc = _NC_CACHE["nc"]

    w_ada2 = np.ascontiguousarray(np.asarray(w_ada, f32)[0])
    b_ada1 = np.asarray(b_ada, f32)[0]
    w_in2 = np.ascontiguousarray(np.asarray(w_in, f32)[0])
    w_out2 = np.ascontiguousarray(np.asarray(w_out, f32)[0])
    w_fm = np.ascontiguousarray(np.asarray(w_fmix, f32)[0])
    cw = np.asarray(conv_w, f32)[0]
    cb = np.asarray(conv_b, f32)[0]
    g1 = np.asarray(ln_g, f32)[0]
    b1 = np.asarray(ln_b, f32)[0]

    b_fm = np.ascontiguousarray(np.repeat(b_ada1[:2048].reshape(16, 128).T[:, :, None], 2, axis=2)).astype(f32)
    bcast = np.ascontiguousarray(np.broadcast_to(np.stack([b_ada1[2048:], g1, b1])[None], (128, 3, 1024))).astype(f32)
    convp = np.ascontiguousarray(np.concatenate([cw, cb[None]], axis=0).reshape(4, 8, 128).transpose(2, 1, 0)).astype(f32)
    d = np.arange(128)
    ang = 2 * np.pi * ((d[:, None] * d[None, :]) % 128) / 128
    cdt = np.ascontiguousarray(np.stack([np.cos(ang), np.sin(ang)], axis=1) / np.sqrt(128.0)).astype(f32)

    tabs = {}
    for r in range(2):
        tabs[("S", r)] = _tables(8192, r)
        tabs[("P", r)] = _tables(4096, r)

    in_maps = []
    for c in range(8):
        j, r = c % 4, c // 4
        m = {}
        xs = x_sample[j]
        xp = x_prompt[j]
        m["xTp_S"] = np.ascontiguousarray(xs.reshape(128, 64, 1024).transpose(2, 1, 0).reshape(1024, 8192))
        m["xTp_P"] = np.ascontiguousarray(xp.reshape(128, 32, 1024).transpose(2, 1, 0).reshape(1024, 4096))
        xso = xs[r * 4096:(r + 1) * 4096]
        xpo = xp[r * 2048:(r + 1) * 2048]
        m["xTo_S"] = np.ascontiguousarray(xso.T)
        m["xTo_P"] = np.ascontiguousarray(xpo.T)
        m["xo_S"] = np.ascontiguousarray(xso)
        m["xo_P"] = np.ascontiguousarray(xpo)
        xh = np.zeros((32, 1024), f32)
        hm = np.zeros((32,), f32)
        for (base, xfull, t0, ntile) in ((0, xs, r * 4096, 8), (16, xp, r * 2048, 4)):
            Sfull = xfull.shape[0]
            for i in range(ntile):
                for side, tok in ((0, t0 + 512 * i - 1), (1, t0 + 512 * i + 512)):
                    if 0 <= tok < Sfull:
                        xh[base + 2 * i + side] = xfull[tok]
                        hm[base + 2 * i + side] = 1.0
        m["xTh"] = np.ascontiguousarray(xh.T)
        m["hmask"] = np.ascontiguousarray(np.broadcast_to(hm[None, None, :], (128, 8, 32))).astype(f32)
        cc = np.stack([np.asarray(c_sample, f32)[j], np.asarray(c_prompt, f32)[j]], axis=1)
        m["cT"] = np.ascontiguousarray(cc.reshape(8, 128, 2).transpose(1, 0, 2))
        m["w_ada"] = w_ada2
        m["b_fm"] = b_fm
        m["bcast"] = bcast
        m["w_in"] = w_in2
        m["w_out"] = w_out2
        m["w_fmix"] = w_fm
        m["convp"] = convp
        m["cdt"] = cdt
        m["MA_S"], m["MB_S"] = tabs[("S", r)]
        m["MA_P"], m["MB_P"] = tabs[("P", r)]
        in_maps.append(m)

    res = run_bass_kernel_spmd(nc, in_maps, core_ids=list(range(8)))
    y_prompt = np.empty((4, 4096, 1024), f32)
    y_sample = np.empty((4, 8192, 1024), f32)
    for c in range(8):
        j, r = c % 4, c // 4
        y_sample[j, r * 4096:(r + 1) * 4096] = res.results[c]["y_S"]
        y_prompt[j, r * 2048:(r + 1) * 2048] = res.results[c]["y_P"]
    return (y_prompt, y_sample)
```

```python
import numpy as np
import ml_dtypes
from contextlib import ExitStack
import concourse.bass as bass
import concourse.mybir as mybir
from concourse.bass_utils import run_bass_kernel_spmd

F32 = mybir.dt.float32
BF16 = mybir.dt.bfloat16
AF = mybir.ActivationFunctionType
ALU = mybir.AluOpType

ALPHA = float(2.0 ** 0.25)
LN_EPS = 1e-5
SEQS = {"S": dict(S=8192, N2=64, q=0), "P": dict(S=4096, N2=32, q=1)}


class Ev:
    __slots__ = ("sem", "val")

    def __init__(self, sem, val):
        self.sem = sem
        self.val = val


class Prog:
    ENGS = ("sync", "scalar", "vector", "gpsimd", "tensor")

    def __init__(self, nc, stack):
        self.nc = nc
        self.stack = stack
        self.q = {e: [] for e in self.ENGS}
        self.sems = {}
        self.cnt = {}
        self.waited = {e: {} for e in self.ENGS}
        self.wr = {}
        self.rd = {}
        self.latest = {}

    def sem(self, name):
        if name not in self.sems:
            self.sems[name] = self.stack.enter_context(self.nc.semaphore(name))
            self.cnt[name] = 0
        return self.sems[name]

    def _wait(self, eng, sem, val):
        w = self.waited[eng]
        if w.get(sem, 0) >= val:
            return
        w[sem] = val
        s = self.sems[sem]
        self.q[eng].append(lambda e, s=s, v=val: e.wait_ge(s, v))

    def _deps(self, eng, reads, writes, extra):
        need = {}
        for k in reads:
            for s, v in self.wr.get(k, {}).items():
                need[s] = max(need.get(s, 0), v)
        for k in writes:
            for s, v in self.wr.get(k, {}).items():
                need[s] = max(need.get(s, 0), v)
            for s, v in self.rd.get(k, {}).items():
                need[s] = max(need.get(s, 0), v)
        for ev in extra:
            if ev is not None:
                need[ev.sem] = max(need.get(ev.sem, 0), ev.val)
        for s, v in need.items():
            self._wait(eng, s, v)

    def note(self, ev, reads=(), writes=()):
        for k in reads:
            d = self.rd.setdefault(k, {})
            d[ev.sem] = max(d.get(ev.sem, 0), ev.val)
        for k in writes:
            d = self.wr.setdefault(k, {})
            d[ev.sem] = max(d.get(ev.sem, 0), ev.val)
        self.latest[ev.sem] = max(self.latest.get(ev.sem, 0), ev.val)

    def op(self, eng, fn, reads=(), writes=(), extra=()):
        self._deps(eng, reads, writes, extra)
        name = "c_" + eng
        s = self.sem(name)
        self.cnt[name] += 1
        v = self.cnt[name]
        self.q[eng].append(lambda e, fn=fn, s=s: fn(e).then_inc(s, 1))
        ev = Ev(name, v)
        self.note(ev, reads, writes)
        return ev

    def op_steps(self, eng, steps, writes=()):
        self._deps(eng, (), writes, ())
        name = "c_" + eng
        s = self.sem(name)
        n = len(steps)
        all_reads = []
        for i, (fn, reads) in enumerate(steps):
            self._deps(eng, reads, (), ())
            all_reads.extend(reads)
            if i < n - 1:
                self.q[eng].append(lambda e, fn=fn: fn(e))
            else:
                self.cnt[name] += 1
                self.q[eng].append(lambda e, fn=fn, s=s: fn(e).then_inc(s, 1))
        ev = Ev(name, self.cnt[name])
        self.note(ev, all_reads, writes)
        return ev

    def dma(self, eng, out, in_, sem, reads=(), writes=(), extra=(), track=True):
        self._deps(eng, reads, writes, extra)
        s = self.sem(sem)
        self.cnt[sem] += 16
        v = self.cnt[sem]
        self.q[eng].append(lambda e, s=s, out=out, in_=in_: e.dma_start(out=out, in_=in_).then_inc(s, 16))
        ev = Ev(sem, v)
        if track:
            self.note(ev, reads, writes)
        else:
            self.latest[sem] = max(self.latest.get(sem, 0), v)
        return ev

    def barrier(self):
        for eng in self.ENGS:
            for s, v in self.latest.items():
                self._wait(eng, s, v)

    def emit(self):
        nc = self.nc
        qs = self.q
        with nc.Block() as block:
            @block.sync
            def _(e):
                for f in qs["sync"]:
                    f(e)

            @block.scalar
            def _(e):
                for f in qs["scalar"]:
                    f(e)

            @block.vector
            def _(e):
                for f in qs["vector"]:
                    f(e)

            @block.gpsimd
            def _(e):
                for f in qs["gpsimd"]:
                    f(e)

            @block.tensor
            def _(e):
                for f in qs["tensor"]:
                    f(e)
        self.q = {e: [] for e in self.ENGS}


def build_nc(dbg=False):
    nc = bass.Bass("TRN2", target_bir_lowering=False)

    def din(name, shape, dt=F32):
        return nc.dram_tensor(name, list(shape), dt, kind="ExternalInput")

    xTp = {"S": din("xTp_S", [1024, 8192]), "P": din("xTp_P", [1024, 4096])}
    xTo = {"S": din("xTo_S", [1024, 4096]), "P": din("xTo_P", [1024, 2048])}
    xo = {"S": din("xo_S", [4096, 1024]), "P": din("xo_P", [2048, 1024])}
    xTh_d = din("xTh", [1024, 32])
    hmask_d = din("hmask", [128, 8, 32])
    cT_d = din("cT", [128, 8, 2])
    wada_d = din("w_ada", [1024, 3072])
    bfm_d = din("b_fm", [128, 16, 2])
    bcast_d = din("bcast", [128, 3, 1024])
    win_d = din("w_in", [1024, 6144])
    wout_d = din("w_out", [2048, 1024])
    wfm_d = din("w_fmix", [8, 128, 128])
    convp_d = din("convp", [128, 8, 4])
    cdt_d = din("cdt", [128, 2, 128])
    MA_d = {"S": din("MA_S", [64, 128, 128], BF16), "P": din("MA_P", [32, 128, 128], BF16)}
    MB_d = {"S": din("MB_S", [128, 4, 64], BF16), "P": din("MB_P", [64, 4, 32], BF16)}
    y_d = {"S": nc.dram_tensor("y_S", [4096, 1024], F32, kind="ExternalOutput"),
           "P": nc.dram_tensor("y_P", [2048, 1024], F32, kind="ExternalOutput")}
    Gd = {"S": nc.dram_tensor("Gd_S", [2, 64, 64, 1024], BF16, kind="Internal"),
          "P": nc.dram_tensor("Gd_P", [2, 32, 64, 1024], BF16, kind="Internal")}
    FRd = {"S": nc.dram_tensor("FR_S", [8, 128, 4096], BF16, kind="Internal"),
           "P": nc.dram_tensor("FR_P", [8, 128, 2048], BF16, kind="Internal")}
    gate_d = nc.dram_tensor("gate_scr", [128, 2, 1024], F32, kind="Internal")

    with ExitStack() as G:
        _uid = [0]

        def sb(st, name, shape, dt):
            _uid[0] += 1
            return st.enter_context(nc.sbuf_tensor(f"sb{_uid[0]}_{name}", list(shape), dt))

        P = Prog(nc, G)
        ps = G.enter_context(nc.psum_tensor("ps", [128, 8, 512], F32))
        ss = sb(G, "ss", [128, 16, 2], F32)
        convp = sb(G, "convp", [128, 8, 4], F32)
        mb = {k: sb(G, "mb_" + k, list(MB_d[k].shape), BF16) for k in ("S", "P")}

        def mm(out, pairs, reads, writes):
            def fn(e):
                n = len(pairs)
                ins = None
                for i, (l, r) in enumerate(pairs):
                    ins = e.matmul(out, lhsT=l, rhs=r, start=(i == 0), stop=(i == n - 1))
                return ins
            return P.op("tensor", fn, reads=reads, writes=writes)

        def kcview(ap):
            return ap.rearrange("(k p) t -> p k t", p=128)

        with ExitStack() as B:
            wvf_b = sb(B, "wvf_b", [128, 8, 1024], BF16)
            with ExitStack() as C1:
                ma = {k: sb(C1, "ma_" + k, [128, SEQS[k]["N2"], 128], BF16) for k in ("P", "S")}
                xt = [sb(C1, f"xt{i}", [128, 8, 512], F32) for i in range(2)]
                ht = [sb(C1, f"ht{i}", [128, 8, 512], BF16) for i in range(2)]
                vsb = [sb(C1, f"vsb{i}", [128, 1024], BF16) for i in range(2)]
                gsb = [sb(C1, f"gsb{i}", [128, 4, 1024], BF16) for i in range(2)]
                ct = sb(C1, "ct", [128, 8, 2], F32)
                sct = sb(C1, "sct", [128, 8, 2], F32)
                sct_b = sb(C1, "sct_b", [128, 8, 2], BF16)
                ones = sb(C1, "ones", [128, 128], F32)
                rep = sb(C1, "rep", [128, 8, 2, 128], BF16)
                wada = [sb(C1, f"wada{i}", [128, 8, 512], BF16) for i in range(6)]
                bfm = sb(C1, "bfm", [128, 16, 2], F32)
                bc0 = sb(C1, "bc0", [128, 1024], F32)
                gate_sb = sb(C1, "gate_sb", [128, 2, 1024], F32)

                P.dma("sync", ct[:], cT_d.ap(), "d_A", track=False)
                P.dma("sync", bfm[:], bfm_d.ap(), "d_A", track=False)
                P.dma("sync", bc0[:], bcast_d.ap()[:, 0, :], "d_A", track=False)
                P.dma("sync", convp[:], convp_d.ap(), "d_A", track=False)
                P.dma("sync", mb["S"][:], MB_d["S"].ap(), "d_A", track=False)
                evA = P.dma("sync", mb["P"][:], MB_d["P"].ap(), "d_A", track=False)
                P.note(evA, writes=["ct", "bfm", "bc0", "convp", "mb"])
                for i in range(4):
                    P.dma("gpsimd", wada[i][:], kcview(wada_d.ap()[:, i * 512:(i + 1) * 512]), f"d_wa{i}", writes=[("wada", i)])
                for i in range(2):
                    P.dma("gpsimd", wvf_b[:, :, i * 512:(i + 1) * 512], kcview(win_d.ap()[:, 4096 + i * 512:4096 + (i + 1) * 512]),
                          f"d_wvf{i}", writes=[("wvf", i)])
                for sk in ("P", "S"):
                    for i in range(SEQS[sk]["N2"] // 16):
                        P.dma("gpsimd", ma[sk][:, i * 16:(i + 1) * 16, :],
                              MA_d[sk].ap()[i * 16:(i + 1) * 16].rearrange("a p c -> p a c"),
                              f"d_ma{sk}{i}", writes=[("ma", sk, i)])
                for i in range(4, 6):
                    P.dma("gpsimd", wada[i][:], kcview(wada_d.ap()[:, i * 512:(i + 1) * 512]), f"d_wa{i}", writes=[("wada", i)])

                P.op("scalar", lambda e: e.activation(out=sct[:], in_=ct[:], func=AF.Silu), reads=["ct"], writes=["sct"])
                P.op("scalar", lambda e: e.copy(out=sct_b[:], in_=sct[:]), reads=["sct"], writes=["sct_b"])
                P.op("vector", lambda e: e.memset(ones[:], 1.0), writes=["ones"])
                for kc in range(8):
                    for q in range(2):
                        P.op("vector", lambda e, kc=kc, q=q: e.tensor_scalar(
                            out=rep[:, kc, q, :], in0=ones[:], scalar1=sct[:, kc, q:q + 1], scalar2=None, op0=ALU.mult),
                            reads=["ones", "sct"], writes=[("rep", kc, q)])
                for ch in range(16):
                    piece = wada[ch // 4]
                    col = (ch % 4) * 128
                    mm(ps[:, 0, 2 * ch:2 * ch + 2], [(piece[:, kc, col:col + 128], sct_b[:, kc, :]) for kc in range(8)],
                       reads=[("wada", ch // 4), "sct_b"], writes=[("ps", 0)])
                P.op("vector", lambda e: e.tensor_tensor(out=ss[:], in0=ps[:, 0, 0:32].rearrange("p (c q) -> p c q", q=2),
                                                         in1=bfm[:], op=ALU.add),
                     reads=[("ps", 0), "bfm"], writes=["ss"])
                P.op("vector", lambda e: e.tensor_scalar_add(out=ss[:, 8:16, :], in0=ss[:, 8:16, :], scalar1=1.0),
                     reads=["ss"], writes=["ss"])
                batches = [(sk, b) for sk in ("P", "S") for b in range(SEQS[sk]["N2"] // 4)]

                def load_x(bi):
                    sk, b = batches[bi]
                    xb = bi % 2
                    P.dma("sync", xt[xb][:], kcview(xTp[sk].ap()[:, b * 512:(b + 1) * 512]), f"d_xt{xb}",
                          writes=[("xt", xb)])

                def hconv(bi):
                    sk, b = batches[bi]
                    q = SEQS[sk]["q"]
                    xb = bi % 2
                    for kc in range(8):
                        P.op("scalar", lambda e, xb=xb, kc=kc, q=q: e.activation(
                            out=ht[xb][:, kc, :], in_=xt[xb][:, kc, :], func=AF.Identity,
                            scale=ss[:, 8 + kc, q:q + 1], bias=ss[:, kc, q:q + 1]),
                            reads=[("xt", xb), "ss"], writes=[("ht", xb, kc)])

                def stage_a(bi, j):
                    sk, b_ = batches[bi]
                    s2 = 4 * b_ + j
                    vb = (4 * bi + j) % 2
                    gslot = bi % 2
                    par = (4 * bi + j) % 2
                    for half in range(2):
                        bank = 4 + 2 * par + half
                        mm(ps[:, bank, :], [(ma[sk][:, s2, :], vsb[vb][:, half * 512:(half + 1) * 512])],
                           reads=[("ma", sk, s2 // 16), ("vsb", vb, half)], writes=[("ps", bank)])
                        if half == 0:
                            P.op("scalar", lambda e, gslot=gslot, j=j, half=half, bank=bank: e.copy(
                                out=gsb[gslot][:, j, half * 512:(half + 1) * 512], in_=ps[:, bank, :]),
                                reads=[("ps", bank)], writes=[("gsb", gslot, j, half)])
                        else:
                            P.op("vector", lambda e, gslot=gslot, j=j, half=half, bank=bank: e.tensor_copy(
                                out=gsb[gslot][:, j, half * 512:(half + 1) * 512], in_=ps[:, bank, :]),
                                reads=[("ps", bank)], writes=[("gsb", gslot, j, half)])
                    if j == 3:
                        for ri in range(2):
                            P.dma("gpsimd", Gd[sk].ap()[ri, 4 * b_:4 * b_ + 4].rearrange("a k f -> k a f"),
                                  gsb[gslot][ri * 64:(ri + 1) * 64, :, :], f"d_g{gslot}{ri}",
                                  reads=[("gsb", gslot, jj, h) for jj in range(4) for h in range(2)])

                nbt = len(batches)
                load_x(0)
                load_x(1)
                hconv(0)
                prev = None
                for bi in range(nbt):
                    xb = bi % 2
                    if bi + 1 < nbt:
                        hconv(bi + 1)
                    if bi + 2 < nbt:
                        load_x(bi + 2)
                    for j in range(4):
                        vb = (4 * bi + j) % 2
                        for half in range(2):
                            mm(ps[:, vb * 2 + half, :],
                               [(ht[xb][:, kc, j * 128:(j + 1) * 128], wvf_b[:, kc, half * 512:(half + 1) * 512]) for kc in range(8)],
                               reads=[("ht", xb, kc) for kc in range(8)] + [("wvf", half)], writes=[("ps", vb * 2 + half)])
                            P.op("vector", lambda e, vb=vb, half=half: e.tensor_copy(
                                out=vsb[vb][:, half * 512:(half + 1) * 512], in_=ps[:, vb * 2 + half, :]),
                                reads=[("ps", vb * 2 + half)], writes=[("vsb", vb, half)])
                        if prev is not None:
                            stage_a(*prev)
                        prev = (bi, j)
                stage_a(*prev)
                for q in range(2):
                    for half in range(2):
                        bank = 4 + q * 2 + half
                        mm(ps[:, bank, :], [(rep[:, kc, q, :], wada[4 + half][:, kc, :]) for kc in range(8)],
                           reads=[("rep", kc, q) for kc in range(8)] + [("wada", 4 + half)], writes=[("ps", bank)])
                        P.op("vector", lambda e, q=q, half=half, bank=bank: e.tensor_tensor(
                            out=gate_sb[:, q, half * 512:(half + 1) * 512], in0=ps[:, bank, :],
                            in1=bc0[:, half * 512:(half + 1) * 512], op=ALU.add),
                            reads=[("ps", bank), "bc0"], writes=[("gate_sb", q, half)])
                P.dma("gpsimd", gate_d.ap(), gate_sb[:], "d_gs",
                      reads=[("gate_sb", q, h) for q in range(2) for h in range(2)], writes=["gate_d"])
                P.barrier()
                P.emit()

        with ExitStack() as W:
            win_b = sb(W, "win_b", [128, 8, 5120], BF16)
            F2S = ExitStack()
            cw_b = sb(F2S, "cw_b", [128, 2, 8, 128], BF16)
            with ExitStack() as C2:
                gt = [sb(C2, f"gt{i}", [128, 32, 256], BF16) for i in range(3)]
                psb = [sb(C2, f"psb{i}", [128, 2, 32, 128], BF16) for i in range(2)]
                frs = [sb(C2, f"frs{i}", [128, 4096], BF16) for i in range(2)]
                cdt = sb(C2, "cdt", [128, 2, 128], F32)
                wfm = sb(C2, "wfm", [128, 1024], F32)
                P.dma("sync", cdt[:], cdt_d.ap(), "d_C", track=False)
                evC = P.dma("sync", wfm[:].rearrange("p (g e) -> p g e", g=8), wfm_d.ap().rearrange("g m e -> m g e"), "d_C", track=False)
                P.note(evC, writes=["cdt", "wfm"])
                for cs in range(2):
                    for gh in range(2):
                        bank = 4 + cs * 2 + gh
                        mm(ps[:, bank, :], [(cdt[:, cs, :], wfm[:, gh * 512:(gh + 1) * 512])],
                           reads=["cdt", "wfm"], writes=[("ps", bank)])
                        P.op("scalar", lambda e, cs=cs, gh=gh, bank=bank: e.copy(
                            out=cw_b[:, cs, gh * 4:(gh + 1) * 4, :], in_=ps[:, bank, :].rearrange("p (g e) -> p g e", g=4)),
                            reads=[("ps", bank)], writes=["cw_b"])
                for i in range(10):
                    src = i * 512 if i < 8 else 5120 + (i - 8) * 512
                    P.dma("gpsimd", win_b[:, :, i * 512:(i + 1) * 512], kcview(win_d.ap()[:, src:src + 512]), f"d_wr{i % 4}", writes=["wres"])
                gcount = 0
                gtc = 0
                bankc = 0
                cc = 0
                ev_i = 0
                for sk in ("P", "S"):
                    N2 = SEQS[sk]["N2"]
                    K2 = 2 * N2
                    T = 64 * N2
                    H2 = N2 // 2
                    nslot = 512 // N2
                    Gv = Gd[sk].ap().rearrange("r a k f -> (r a) k f")
                    psb_all = {pb: [("psb", pb, kk) for kk in range(128)] for pb in range(2)}
                    for gp in range(4):
                        pbs = [0, 1]
                        gcount += 1
                        for kb in range(2):
                            gi = gtc % 3
                            gtc += 1
                            P.dma("sync", gt[gi][0:K2], Gv[:, kb * 32:(kb + 1) * 32, gp * 256:(gp + 1) * 256], f"d_gt{gi}",
                                  writes=[("gt", gi)])
                            runs = [[(k, k - 32 * kb, 2 if k == 0 else 0) for k in range(32 * kb, 32 * kb + 32)]]
                            if kb == 0:
                                runs.append([(64, 0, 3)])
                                runs.append([(k, 128 - k, 1) for k in range(97, 128)])
                            else:
                                runs.append([(k, 128 - k - 32, 1) for k in range(65, 97)])
                            chunks = []
                            for run in runs:
                                for c0 in range(0, len(run), 2 * nslot):
                                    chunks.append(run[c0:c0 + 2 * nslot])
                            for gg in range(2):
                                pb = pbs[gg]
                                for chunk in chunks:
                                    bset = (bankc % 3) * 2
                                    bankc += 1

                                    def fnb(e, gi=gi, chunk=chunk, bset=bset, N2=N2, K2=K2, nslot=nslot, sk=sk, gg=gg):
                                        ins = None
                                        for idx, (k1o, loc, mat) in enumerate(chunk):
                                            bb, sl = divmod(idx, nslot)
                                            ins = e.matmul(ps[:, bset + bb, sl * N2:(sl + 1) * N2],
                                                           lhsT=gt[gi][0:K2, loc, gg * 128:(gg + 1) * 128],
                                                           rhs=mb[sk][0:K2, mat, :], start=True, stop=True)
                                        return ins
                                    P.op("tensor", fnb, reads=[("gt", gi), "mb"], writes=[("ps", bset), ("ps", bset + 1)])
                                    n = len(chunk)
                                    k1_0 = chunk[0][0]
                                    keys = [("psb", pb, kk) for kk in range(k1_0, k1_0 + n)]
                                    oap = psb[pb][:, :, 0:H2, k1_0:k1_0 + n]
                                    iap = ps[:, bset:bset + 2, :].rearrange("p b (s r k) -> p r k (b s)", s=nslot, r=2)[:, :, :, 0:n]
                                    if ev_i % 2 == 0:
                                        P.op("scalar", lambda e, oap=oap, iap=iap: e.copy(out=oap, in_=iap),
                                             reads=[("ps", bset), ("ps", bset + 1)], writes=keys)
                                    else:
                                        P.op("vector", lambda e, oap=oap, iap=iap: e.tensor_copy(out=oap, in_=iap),
                                             reads=[("ps", bset), ("ps", bset + 1)], writes=keys)
                                    ev_i += 1
                        for gg in range(2):
                            g = gp * 2 + gg
                            pb = pbs[gg]
                            fb = pb
                            frkeys = []
                            for j in range(T // 512):
                                bank = 6 + cc % 2
                                cc += 1
                                mm(ps[:, bank, :],
                                   [(cw_b[:, cs, g, :], psb[pb][:, cs, 4 * j:4 * j + 4, :].rearrange("p a b -> p (a b)")) for cs in range(2)],
                                   reads=psb_all[pb] + ["cw_b"], writes=[("ps", bank)])
                                key = ("frs", fb, j)
                                frkeys.append(key)
                                if ev_i % 2 == 0:
                                    P.op("scalar", lambda e, fb=fb, j=j, bank=bank: e.copy(out=frs[fb][:, j * 512:(j + 1) * 512], in_=ps[:, bank, :]),
                                         reads=[("ps", bank)], writes=[key])
                                else:
                                    P.op("vector", lambda e, fb=fb, j=j, bank=bank: e.tensor_copy(out=frs[fb][:, j * 512:(j + 1) * 512], in_=ps[:, bank, :]),
                                         reads=[("ps", bank)], writes=[key])
                                ev_i += 1
                            P.dma("scalar", FRd[sk].ap()[g], frs[fb][:, 0:T], f"d_fr{fb}", reads=frkeys)
                P.barrier()
                P.emit()
            F2S.close()
            with ExitStack() as M:
                wout_b = sb(M, "wout_b", [128, 16, 1024], BF16)
                for kh in range(2):
                    for cq in range(2):
                        P.dma("gpsimd", wout_b[:, kh * 8:(kh + 1) * 8, cq * 512:(cq + 1) * 512],
                              kcview(wout_d.ap()[kh * 1024:(kh + 1) * 1024, cq * 512:(cq + 1) * 512]), f"d_wo{kh * 2 + cq}", writes=["wout"])
                xt = sb(M, "m_xt", [128, 8, 512], F32)
                ht = sb(M, "m_ht", [128, 8, 512], BF16)
                yt = sb(M, "m_yt", [128, 16, 512], BF16)
                fr = sb(M, "m_fr", [128, 2, 512], BF16)
                xtok = sb(M, "m_xtok", [128, 1, 1024], F32)
                rbuf = sb(M, "m_rbuf", [128, 2, 1024], F32)
                ybuf = sb(M, "m_ybuf", [128, 2, 1024], F32)
                bc = sb(M, "m_bc", [128, 3, 1024], F32)
                cgsb = sb(M, "m_cgsb", [128, 2, 512], F32)
                sg = sb(M, "m_sg", [128, 2, 512], F32)
                uext = sb(M, "m_uext", [128, 514], F32)
                acc = sb(M, "m_acc", [128, 512], F32)
                y1 = sb(M, "m_y1", [128, 1, 512], F32)
                xth = sb(M, "m_xth", [128, 8, 32], F32)
                hth = sb(M, "m_hth", [128, 8, 32], BF16)
                cgh = sb(M, "m_cgh", [128, 8, 32], F32)
                uh = sb(M, "m_uh", [128, 8, 32], F32)
                hmask = sb(M, "m_hmask", [128, 8, 32], F32)
                stats = sb(M, "m_stats", [128, 12], F32)
                mv = sb(M, "m_mv", [128, 2], F32)
                rstd = sb(M, "m_rstd", [128, 1], F32)
                epst = sb(M, "m_epst", [128, 1], F32)
                nmr = sb(M, "m_nmr", [128, 1], F32)
                P.op("vector", lambda e: e.memset(epst[:], LN_EPS), writes=["epst"])

                P.dma("sync", xth[:], kcview(xTh_d.ap()), "d_M", track=False)
                P.dma("sync", hmask[:], hmask_d.ap(), "d_M", track=False)
                evM = P.dma("sync", bc[:, 1:3, :], bcast_d.ap()[:, 1:3, :], "d_M", track=False)
                P.note(evM, writes=["xth", "hmask", "lngb"])
                for kc in range(8):
                    for q, (c0, c1) in ((0, (0, 16)), (1, (16, 32))):
                        P.op("scalar", lambda e, kc=kc, q=q, c0=c0, c1=c1: e.activation(
                            out=hth[:, kc, c0:c1], in_=xth[:, kc, c0:c1], func=AF.Identity,
                            scale=ss[:, 8 + kc, q:q + 1], bias=ss[:, kc, q:q + 1]),
                            reads=["xth", "ss"], writes=[("hth", kc, q)])
                hth_keys = [("hth", kc, q) for kc in range(8) for q in range(2)]
                for c in range(8):
                    mm(ps[:, 0, c * 32:(c + 1) * 32], [(win_b[:, kc, 1024 + c * 128:1024 + (c + 1) * 128], hth[:, kc, :]) for kc in range(8)],
                       reads=hth_keys + ["wres"], writes=[("ps", 0)])
                    mm(ps[:, 1, c * 32:(c + 1) * 32], [(win_b[:, kc, 2048 + c * 128:2048 + (c + 1) * 128], hth[:, kc, :]) for kc in range(8)],
                       reads=hth_keys + ["wres"], writes=[("ps", 1)])
                P.op("scalar", lambda e: e.copy(out=cgh[:], in_=ps[:, 0, 0:256].rearrange("p (c t) -> p c t", c=8)),
                     reads=[("ps", 0)], writes=["cgh"])
                P.op("vector", lambda e: e.tensor_tensor(out=uh[:], in0=cgh[:], in1=ps[:, 1, 0:256].rearrange("p (c t) -> p c t", c=8), op=ALU.mult),
                     reads=["cgh", ("ps", 1)], writes=["uh"])
                P.op("vector", lambda e: e.tensor_tensor(out=uh[:], in0=uh[:], in1=hmask[:], op=ALU.mult),
                     reads=["uh", "hmask"], writes=["uh"])

                hbase = {"S": 0, "P": 16}
                tiles = [(sk, i) for sk in ("P", "S") for i in range(SEQS[sk]["S"] // 2 // 512)]

                def load_xt(ti):
                    sk, i = tiles[ti]
                    P.dma("sync", xt[:], kcview(xTo[sk].ap()[:, i * 512:(i + 1) * 512]), "d_mxt", writes=["xt"])

                def hconv_m(ti):
                    sk, i = tiles[ti]
                    q = SEQS[sk]["q"]
                    for kc in range(8):
                        P.op("scalar", lambda e, kc=kc, q=q: e.activation(
                            out=ht[:, kc, :], in_=xt[:, kc, :], func=AF.Identity,
                            scale=ss[:, 8 + kc, q:q + 1], bias=ss[:, kc, q:q + 1]),
                            reads=["xt", "ss"], writes=[("ht", kc)])

                ht_keys = [("ht", kc) for kc in range(8)]
                yt_keys = [("yt", kc) for kc in range(16)]
                load_xt(0)
                hconv_m(0)
                for ti, (sk, i) in enumerate(tiles):
                    q = SEQS[sk]["q"]
                    if ti + 1 < len(tiles):
                        load_xt(ti + 1)
                    hl = hbase[sk] + 2 * i
                    for c in range(8):
                        st_ = (c % 2) * 4
                        tb = c % 2
                        for wi, off in enumerate((0, 1024, 2048, 3072)):
                            mm(ps[:, st_ + wi, :],
                               [(win_b[:, kc, off + c * 128:off + (c + 1) * 128], ht[:, kc, :]) for kc in range(8)],
                               reads=ht_keys + ["wres"], writes=[("ps", st_ + wi)])
                        P.op("scalar", lambda e, tb=tb, st_=st_: e.copy(out=cgsb[:, tb, :], in_=ps[:, st_ + 1, :]),
                             reads=[("ps", st_ + 1)], writes=[("cgsb", tb)])
                        P.op("scalar", lambda e, tb=tb, st_=st_: e.activation(out=sg[:, tb, :], in_=ps[:, st_ + 3, :], func=AF.Silu),
                             reads=[("ps", st_ + 3)], writes=[("sg", tb)])
                        P.op("gpsimd", lambda e, c=c, hl=hl: e.tensor_copy(out=uext[:, 0:1], in_=uh[:, c, hl:hl + 1]),
                             reads=["uh"], writes=["uextL"])
                        P.op("gpsimd", lambda e, c=c, hl=hl: e.tensor_copy(out=uext[:, 513:514], in_=uh[:, c, hl + 1:hl + 2]),
                             reads=["uh"], writes=["uextR"])
                        P.op("vector", lambda e, tb=tb, st_=st_: e.tensor_tensor(out=uext[:, 1:513], in0=cgsb[:, tb, :], in1=ps[:, st_ + 2, :], op=ALU.mult),
                             reads=[("cgsb", tb), ("ps", st_ + 2)], writes=["uextM"])
                        P.op("vector", lambda e, c=c: e.tensor_scalar(out=acc[:], in0=uext[:, 1:513], scalar1=convp[:, c, 1:2], scalar2=convp[:, c, 3:4],
                                                                     op0=ALU.mult, op1=ALU.add),
                             reads=["uextM", "convp"], writes=["acc"])
                        P.op("vector", lambda e, c=c: e.scalar_tensor_tensor(out=acc[:], in0=uext[:, 0:512], scalar=convp[:, c, 0:1], in1=acc[:],
                                                                            op0=ALU.mult, op1=ALU.add),
                             reads=["uextM", "uextL", "acc"], writes=["acc"])
                        P.op("vector", lambda e, c=c: e.scalar_tensor_tensor(out=acc[:], in0=uext[:, 2:514], scalar=convp[:, c, 2:3], in1=acc[:],
                                                                            op0=ALU.mult, op1=ALU.add),
                             reads=["uextM", "uextR", "acc"], writes=["acc"])
                        P.op("vector", lambda e, tb=tb, st_=st_: e.tensor_tensor(out=y1[:, 0, :], in0=acc[:], in1=ps[:, st_, :], op=ALU.mult),
                             reads=["acc", ("ps", st_)], writes=["y1"])
                        P.op("gpsimd", lambda e, tb=tb, c=c: e.tensor_tensor(out=yt[:, c, :], in0=y1[:, 0, :], in1=sg[:, tb, :], op=ALU.mult),
                             reads=["y1", ("sg", tb)], writes=[("yt", c)])
                    for c in range(8):
                        bank = c
                        fb = c % 2
                        mm(ps[:, bank, :],
                           [(win_b[:, kc, 4096 + c * 128:4096 + (c + 1) * 128], ht[:, kc, :]) for kc in range(8)],
                           reads=ht_keys + ["wres"], writes=[("ps", bank)])
                        P.dma("sync", fr[:, fb, :], FRd[sk].ap()[c, :, i * 512:(i + 1) * 512], f"d_frl{fb}", writes=[("fr", fb)])
                        P.op("scalar", lambda e, fb=fb, bank=bank: e.activation(out=cgsb[:, fb, :], in_=ps[:, bank, :], func=AF.Silu),
                             reads=[("ps", bank)], writes=[("cgsb", fb)])
                        P.op("gpsimd", lambda e, fb=fb, c=c: e.tensor_tensor(out=yt[:, 8 + c, :], in0=cgsb[:, fb, :], in1=fr[:, fb, :], op=ALU.mult),
                             reads=[("cgsb", fb), ("fr", fb)], writes=[("yt", 8 + c)])
                    if ti + 1 < len(tiles):
                        hconv_m(ti + 1)
                    if i == 0:
                        P.dma("sync", bc[:, 0, :], gate_d.ap()[:, q, :], "d_gate", reads=["gate_d"], writes=["gate"])
                    def o_mm(sub):
                        ob = (sub % 2) * 4
                        for half in range(2):
                            steps = []
                            for kc in range(16):
                                steps.append((lambda e, kc=kc, sub=sub, half=half, ob=ob: e.matmul(
                                    ps[:, ob + half, :], lhsT=yt[:, kc, sub * 128:(sub + 1) * 128],
                                    rhs=wout_b[:, kc, half * 512:(half + 1) * 512], start=(kc == 0), stop=(kc == 15)),
                                    [("yt", kc), "wout"]))
                            P.op_steps("tensor", steps, writes=[("ps", ob + half)])

                    def o_t(sub):
                        ob = (sub % 2) * 4
                        rb = sub % 2
                        P.op("vector", lambda e, ob=ob, rb=rb: e.tensor_tensor(out=rbuf[:, rb, :].rearrange("p (a b) -> p a b", a=2), in0=ps[:, ob:ob + 2, :],
                                                                             in1=bc[:, 0, :].rearrange("p (a b) -> p a b", a=2), op=ALU.mult),
                             reads=[("ps", ob), ("ps", ob + 1), "gate"], writes=[("rbuf", rb)])

                    def o_chain(sub):
                        rb = sub % 2
                        r0 = i * 512 + sub * 128
                        P.dma("sync", xtok[:, 0, :], xo[sk].ap()[r0:r0 + 128, :], "d_xtok", writes=["xtok"])
                        P.op("vector", lambda e, rb=rb: e.scalar_tensor_tensor(out=rbuf[:, rb, :], in0=xtok[:, 0, :], scalar=ALPHA, in1=rbuf[:, rb, :],
                                                                              op0=ALU.mult, op1=ALU.add),
                             reads=["xtok", ("rbuf", rb)], writes=[("rbuf", rb)])
                        for hh in range(2):
                            P.op("vector", lambda e, hh=hh, rb=rb: e.bn_stats(out=stats[:, hh * 6:(hh + 1) * 6], in_=rbuf[:, rb, hh * 512:(hh + 1) * 512]),
                                 reads=[("rbuf", rb)], writes=[("stats", hh)])
                        P.op("vector", lambda e: e.bn_aggr(out=mv[:], in_=stats[:]), reads=[("stats", 0), ("stats", 1)], writes=["mv"])
                        P.op("scalar", lambda e: e.activation(out=rstd[:], in_=mv[:, 1:2], func=AF.Sqrt, bias=epst[:, 0:1], scale=1.0),
                             reads=["mv", "epst"], writes=["rstd"])
                        P.op("vector", lambda e: e.reciprocal(out=rstd[:], in_=rstd[:]),
                             reads=["rstd"], writes=["rstd"])
                        P.op("vector", lambda e: e.tensor_scalar(out=nmr[:], in0=mv[:, 0:1], scalar1=-1.0, scalar2=rstd[:, 0:1], op0=ALU.mult, op1=ALU.mult),
                             reads=["mv", "rstd"], writes=["nmr"])
                        P.op("scalar", lambda e, rb=rb: e.activation(out=ybuf[:, rb, :], in_=rbuf[:, rb, :], func=AF.Identity,
                                                                    scale=rstd[:, 0:1], bias=nmr[:, 0:1]),
                             reads=[("rbuf", rb), "rstd", "nmr"], writes=[("ybuf", rb)])
                        P.op("gpsimd", lambda e, rb=rb: e.tensor_tensor(out=ybuf[:, rb, :], in0=ybuf[:, rb, :], in1=bc[:, 1, :], op=ALU.mult),
                             reads=[("ybuf", rb), "lngb"], writes=[("ybuf", rb)])
                        P.op("gpsimd", lambda e, rb=rb: e.tensor_tensor(out=ybuf[:, rb, :], in0=ybuf[:, rb, :], in1=bc[:, 2, :], op=ALU.add),
                             reads=[("ybuf", rb), "lngb"], writes=[("ybuf", rb)])
                        P.dma("gpsimd", y_d[sk].ap()[r0:r0 + 128, :], ybuf[:, rb, :], f"d_y{rb}", reads=[("ybuf", rb)])

                    for sub in range(4):
                        o_mm(sub)
                        o_t(sub)
                        o_chain(sub)
                P.barrier()
                P.emit()
    return nc


def _tables(S, r):
    N2 = S // 128
    s1 = np.arange(128)
    s2 = np.arange(N2)
    k1p = np.arange(64)
    s = (N2 * s1[None, :, None] + s2[:, None, None])
    ang = 2 * np.pi * ((k1p[None, None, :] * s) % S) / S
    re = np.cos(ang)
    im = -np.sin(ang)
    im[:, :, 0] = ((-1.0) ** s1)[None, :]
    MA = np.concatenate([re, im], axis=2) / np.sqrt(128.0)
    H2 = N2 // 2
    k2 = r * H2 + np.arange(H2)
    th = 2 * np.pi * ((k2[None, :] * s2[:, None]) % N2) / N2
    c, sn = np.cos(th), np.sin(th)
    Md = np.zeros((2, N2, 2, H2))
    Md[0, :, 0] = c
    Md[0, :, 1] = -sn
    Md[1, :, 0] = sn
    Md[1, :, 1] = c
    th1 = 2 * np.pi * (((k2[None, :] + 1) * s2[:, None]) % N2) / N2
    c1, s1_ = np.cos(th1), np.sin(th1)
    Mm = np.zeros((2, N2, 2, H2))
    Mm[0, :, 0] = c1
    Mm[0, :, 1] = -s1_
    Mm[1, :, 0] = -s1_
    Mm[1, :, 1] = -c1
    Md0 = Md.copy()
    Md0[1] = 0
    th64 = th + np.pi * s2[:, None] / N2
    M64 = np.zeros((2, N2, 2, H2))
    M64[1, :, 0] = np.cos(th64)
    M64[1, :, 1] = -np.sin(th64)
    MB4 = np.stack([Md, Mm, Md0, M64], axis=0).reshape(4, 2 * N2, N2) / np.sqrt(float(N2))
    return (np.ascontiguousarray(MA).astype(ml_dtypes.bfloat16),
            np.ascontiguousarray(MB4.transpose(1, 0, 2)).astype(ml_dtypes.bfloat16))


_NC_CACHE = {}


def kernel(x_prompt, x_sample, c_prompt, c_sample, w_ada, b_ada, w_in, conv_w, conv_b,
           w_fmix, w_out, ln_g, ln_b):
    f32 = np.float32
    x_prompt = np.asarray(x_prompt, f32)
    x_sample = np.asarray(x_sample, f32)
    if "nc" not in _NC_CACHE:
        _NC_CACHE["nc"] = build_nc()
    nc = _NC_CACHE["nc"]

    w_ada2 = np.ascontiguousarray(np.asarray(w_ada, f32)[0])
    b_ada1 = np.asarray(b_ada, f32)[0]
    w_in2 = np.ascontiguousarray(np.asarray(w_in, f32)[0])
    w_out2 = np.ascontiguousarray(np.asarray(w_out, f32)[0])
    w_fm = np.ascontiguousarray(np.asarray(w_fmix, f32)[0])
    cw = np.asarray(conv_w, f32)[0]
    cb = np.asarray(conv_b, f32)[0]
    g1 = np.asarray(ln_g, f32)[0]
    b1 = np.asarray(ln_b, f32)[0]

    b_fm = np.ascontiguousarray(np.repeat(b_ada1[:2048].reshape(16, 128).T[:, :, None], 2, axis=2)).astype(f32)
    bcast = np.ascontiguousarray(np.broadcast_to(np.stack([b_ada1[2048:], g1, b1])[None], (128, 3, 1024))).astype(f32)
    convp = np.ascontiguousarray(np.concatenate([cw, cb[None]], axis=0).reshape(4, 8, 128).transpose(2, 1, 0)).astype(f32)
    d = np.arange(128)
    ang = 2 * np.pi * ((d[:, None] * d[None, :]) % 128) / 128
    cdt = np.ascontiguousarray(np.stack([np.cos(ang), np.sin(ang)], axis=1) / np.sqrt(128.0)).astype(f32)

    tabs = {}
    for r in range(2):
        tabs[("S", r)] = _tables(8192, r)
        tabs[("P", r)] = _tables(4096, r)

    in_maps = []
    for c in range(8):
        j, r = c % 4, c // 4
        m = {}
        xs = x_sample[j]
        xp = x_prompt[j]
        m["xTp_S"] = np.ascontiguousarray(xs.reshape(128, 64, 1024).transpose(2, 1, 0).reshape(1024, 8192))
        m["xTp_P"] = np.ascontiguousarray(xp.reshape(128, 32, 1024).transpose(2, 1, 0).reshape(1024, 4096))
        xso = xs[r * 4096:(r + 1) * 4096]
        xpo = xp[r * 2048:(r + 1) * 2048]
        m["xTo_S"] = np.ascontiguousarray(xso.T)
        m["xTo_P"] = np.ascontiguousarray(xpo.T)
        m["xo_S"] = np.ascontiguousarray(xso)
        m["xo_P"] = np.ascontiguousarray(xpo)
        xh = np.zeros((32, 1024), f32)
        hm = np.zeros((32,), f32)
        for (base, xfull, t0, ntile) in ((0, xs, r * 4096, 8), (16, xp, r * 2048, 4)):
            Sfull = xfull.shape[0]
            for i in range(ntile):
                for side, tok in ((0, t0 + 512 * i - 1), (1, t0 + 512 * i + 512)):
                    if 0 <= tok < Sfull:
                        xh[base + 2 * i + side] = xfull[tok]
                        hm[base + 2 * i + side] = 1.0
        m["xTh"] = np.ascontiguousarray(xh.T)
        m["hmask"] = np.ascontiguousarray(np.broadcast_to(hm[None, None, :], (128, 8, 32))).astype(f32)
        cc = np.stack([np.asarray(c_sample, f32)[j], np.asarray(c_prompt, f32)[j]], axis=1)
        m["cT"] = np.ascontiguousarray(cc.reshape(8, 128, 2).transpose(1, 0, 2))
        m["w_ada"] = w_ada2
        m["b_fm"] = b_fm
        m["bcast"] = bcast
        m["w_in"] = w_in2
        m["w_out"] = w_out2
        m["w_fmix"] = w_fm
        m["convp"] = convp
        m["cdt"] = cdt
        m["MA_S"], m["MB_S"] = tabs[("S", r)]
        m["MA_P"], m["MB_P"] = tabs[("P", r)]
        in_maps.append(m)

    res = run_bass_kernel_spmd(nc, in_maps, core_ids=list(range(8)))
    y_prompt = np.empty((4, 4096, 1024), f32)
    y_sample = np.empty((4, 8192, 1024), f32)
    for c in range(8):
        j, r = c % 4, c // 4
        y_sample[j, r * 4096:(r + 1) * 4096] = res.results[c]["y_S"]
        y_prompt[j, r * 2048:(r + 1) * 2048] = res.results[c]["y_P"]
    return (y_prompt, y_sample)
```

```python
import numpy as np
import ml_dtypes
from contextlib import ExitStack
import concourse.bass as bass
import concourse.mybir as mybir
from concourse.bass_utils import run_bass_kernel_spmd

F32 = mybir.dt.float32
BF16 = mybir.dt.bfloat16
AF = mybir.ActivationFunctionType
ALU = mybir.AluOpType

ALPHA = float(2.0 ** 0.25)
LN_EPS = 1e-5
SEQS = {"S": dict(S=8192, N2=64, q=0), "P": dict(S=4096, N2=32, q=1)}


class Ev:
    __slots__ = ("sem", "val")

    def __init__(self, sem, val):
        self.sem = sem
        self.val = val


class Prog:
    ENGS = ("sync", "scalar", "vector", "gpsimd", "tensor")

    def __init__(self, nc, stack):
        self.nc = nc
        self.stack = stack
        self.q = {e: [] for e in self.ENGS}
        self.sems = {}
        self.cnt = {}
        self.waited = {e: {} for e in self.ENGS}
        self.wr = {}
        self.rd = {}
        self.latest = {}

    def sem(self, name):
        if name not in self.sems:
            self.sems[name] = self.stack.enter_context(self.nc.semaphore(name))
            self.cnt[name] = 0
        return self.sems[name]

    def _wait(self, eng, sem, val):
        w = self.waited[eng]
        if w.get(sem, 0) >= val:
            return
        w[sem] = val
        s = self.sems[sem]
        self.q[eng].append(lambda e, s=s, v=val: e.wait_ge(s, v))

    def _deps(self, eng, reads, writes, extra):
        need = {}
        for k in reads:
            for s, v in self.wr.get(k, {}).items():
                need[s] = max(need.get(s, 0), v)
        for k in writes:
            for s, v in self.wr.get(k, {}).items():
                need[s] = max(need.get(s, 0), v)
            for s, v in self.rd.get(k, {}).items():
                need[s] = max(need.get(s, 0), v)
        for ev in extra:
            if ev is not None:
                need[ev.sem] = max(need.get(ev.sem, 0), ev.val)
        for s, v in need.items():
            self._wait(eng, s, v)

    def note(self, ev, reads=(), writes=()):
        for k in reads:
            d = self.rd.setdefault(k, {})
            d[ev.sem] = max(d.get(ev.sem, 0), ev.val)
        for k in writes:
            d = self.wr.setdefault(k, {})
            d[ev.sem] = max(d.get(ev.sem, 0), ev.val)
        self.latest[ev.sem] = max(self.latest.get(ev.sem, 0), ev.val)

    def op(self, eng, fn, reads=(), writes=(), extra=()):
        self._deps(eng, reads, writes, extra)
        name = "c_" + eng
        s = self.sem(name)
        self.cnt[name] += 1
        v = self.cnt[name]
        self.q[eng].append(lambda e, fn=fn, s=s: fn(e).then_inc(s, 1))
        ev = Ev(name, v)
        self.note(ev, reads, writes)
        return ev

    def op_steps(self, eng, steps, writes=()):
        self._deps(eng, (), writes, ())
        name = "c_" + eng
        s = self.sem(name)
        n = len(steps)
        all_reads = []
        for i, (fn, reads) in enumerate(steps):
            self._deps(eng, reads, (), ())
            all_reads.extend(reads)
            if i < n - 1:
                self.q[eng].append(lambda e, fn=fn: fn(e))
            else:
                self.cnt[name] += 1
                self.q[eng].append(lambda e, fn=fn, s=s: fn(e).then_inc(s, 1))
        ev = Ev(name, self.cnt[name])
        self.note(ev, all_reads, writes)
        return ev

    def dma(self, eng, out, in_, sem, reads=(), writes=(), extra=(), track=True):
        self._deps(eng, reads, writes, extra)
        s = self.sem(sem)
        self.cnt[sem] += 16
        v = self.cnt[sem]
        self.q[eng].append(lambda e, s=s, out=out, in_=in_: e.dma_start(out=out, in_=in_).then_inc(s, 16))
        ev = Ev(sem, v)
        if track:
            self.note(ev, reads, writes)
        else:
            self.latest[sem] = max(self.latest.get(sem, 0), v)
        return ev

    def barrier(self):
        for eng in self.ENGS:
            for s, v in self.latest.items():
                self._wait(eng, s, v)

    def emit(self):
        nc = self.nc
        qs = self.q
        with nc.Block() as block:
            @block.sync
            def _(e):
                for f in qs["sync"]:
                    f(e)

            @block.scalar
            def _(e):
                for f in qs["scalar"]:
                    f(e)

            @block.vector
            def _(e):
                for f in qs["vector"]:
                    f(e)

            @block.gpsimd
            def _(e):
                for f in qs["gpsimd"]:
                    f(e)

            @block.tensor
            def _(e):
                for f in qs["tensor"]:
                    f(e)
        self.q = {e: [] for e in self.ENGS}


def build_nc(dbg=False):
    nc = bass.Bass("TRN2", target_bir_lowering=False)

    def din(name, shape, dt=F32):
        return nc.dram_tensor(name, list(shape), dt, kind="ExternalInput")

    xTp = {"S": din("xTp_S", [1024, 8192]), "P": din("xTp_P", [1024, 4096])}
    xTo = {"S": din("xTo_S", [1024, 4096]), "P": din("xTo_P", [1024, 2048])}
    xo = {"S": din("xo_S", [4096, 1024]), "P": din("xo_P", [2048, 1024])}
    xTh_d = din("xTh", [1024, 32])
    hmask_d = din("hmask", [128, 8, 32])
    cT_d = din("cT", [128, 8, 2])
    wada_d = din("w_ada", [1024, 3072])
    bfm_d = din("b_fm", [128, 16, 2])
    bcast_d = din("bcast", [128, 3, 1024])
    win_d = din("w_in", [1024, 6144])
    wout_d = din("w_out", [2048, 1024])
    wfm_d = din("w_fmix", [8, 128, 128])
    convp_d = din("convp", [128, 8, 4])
    cdt_d = din("cdt", [128, 2, 128])
    MA_d = {"S": din("MA_S", [64, 128, 128], BF16), "P": din("MA_P", [32, 128, 128], BF16)}
    MB_d = {"S": din("MB_S", [128, 4, 64], BF16), "P": din("MB_P", [64, 4, 32], BF16)}
    y_d = {"S": nc.dram_tensor("y_S", [4096, 1024], F32, kind="ExternalOutput"),
           "P": nc.dram_tensor("y_P", [2048, 1024], F32, kind="ExternalOutput")}
    Gd = {"S": nc.dram_tensor("Gd_S", [2, 64, 64, 1024], BF16, kind="Internal"),
          "P": nc.dram_tensor("Gd_P", [2, 32, 64, 1024], BF16, kind="Internal")}
    FRd = {"S": nc.dram_tensor("FR_S", [8, 128, 4096], BF16, kind="Internal"),
           "P": nc.dram_tensor("FR_P", [8, 128, 2048], BF16, kind="Internal")}
    gate_d = nc.dram_tensor("gate_scr", [128, 2, 1024], F32, kind="Internal")

    with ExitStack() as G:
        _uid = [0]

        def sb(st, name, shape, dt):
            _uid[0] += 1
            return st.enter_context(nc.sbuf_tensor(f"sb{_uid[0]}_{name}", list(shape), dt))

        P = Prog(nc, G)
        ps = G.enter_context(nc.psum_tensor("ps", [128, 8, 512], F32))
        ss = sb(G, "ss", [128, 16, 2], F32)
        convp = sb(G, "convp", [128, 8, 4], F32)
        mb = {k: sb(G, "mb_" + k, list(MB_d[k].shape), BF16) for k in ("S", "P")}

        def mm(out, pairs, reads, writes):
            def fn(e):
                n = len(pairs)
                ins = None
                for i, (l, r) in enumerate(pairs):
                    ins = e.matmul(out, lhsT=l, rhs=r, start=(i == 0), stop=(i == n - 1))
                return ins
            return P.op("tensor", fn, reads=reads, writes=writes)

        def kcview(ap):
            return ap.rearrange("(k p) t -> p k t", p=128)

        with ExitStack() as B:
            wvf_b = sb(B, "wvf_b", [128, 8, 1024], BF16)
            with ExitStack() as C1:
                ma = {k: sb(C1, "ma_" + k, [128, SEQS[k]["N2"], 128], BF16) for k in ("P", "S")}
                xt = [sb(C1, f"xt{i}", [128, 8, 512], F32) for i in range(2)]
                ht = [sb(C1, f"ht{i}", [128, 8, 512], BF16) for i in range(2)]
                vsb = [sb(C1, f"vsb{i}", [128, 1024], BF16) for i in range(2)]
                gsb = [sb(C1, f"gsb{i}", [128, 4, 1024], BF16) for i in range(2)]
                ct = sb(C1, "ct", [128, 8, 2], F32)
                sct = sb(C1, "sct", [128, 8, 2], F32)
                sct_b = sb(C1, "sct_b", [128, 8, 2], BF16)
                ones = sb(C1, "ones", [128, 128], F32)
                rep = sb(C1, "rep", [128, 8, 2, 128], BF16)
                wada = [sb(C1, f"wada{i}", [128, 8, 512], BF16) for i in range(6)]
                bfm = sb(C1, "bfm", [128, 16, 2], F32)
                bc0 = sb(C1, "bc0", [128, 1024], F32)
                gate_sb = sb(C1, "gate_sb", [128, 2, 1024], F32)

                P.dma("sync", ct[:], cT_d.ap(), "d_A", track=False)
                P.dma("sync", bfm[:], bfm_d.ap(), "d_A", track=False)
                P.dma("sync", bc0[:], bcast_d.ap()[:, 0, :], "d_A", track=False)
                P.dma("sync", convp[:], convp_d.ap(), "d_A", track=False)
                P.dma("sync", mb["S"][:], MB_d["S"].ap(), "d_A", track=False)
                evA = P.dma("sync", mb["P"][:], MB_d["P"].ap(), "d_A", track=False)
                P.note(evA, writes=["ct", "bfm", "bc0", "convp", "mb"])
                for i in range(4):
                    P.dma("gpsimd", wada[i][:], kcview(wada_d.ap()[:, i * 512:(i + 1) * 512]), f"d_wa{i}", writes=[("wada", i)])
                for i in range(2):
                    P.dma("gpsimd", wvf_b[:, :, i * 512:(i + 1) * 512], kcview(win_d.ap()[:, 4096 + i * 512:4096 + (i + 1) * 512]),
                          f"d_wvf{i}", writes=[("wvf", i)])
                for sk in ("P", "S"):
                    for i in range(SEQS[sk]["N2"] // 16):
                        P.dma("gpsimd", ma[sk][:, i * 16:(i + 1) * 16, :],
                              MA_d[sk].ap()[i * 16:(i + 1) * 16].rearrange("a p c -> p a c"),
                              f"d_ma{sk}{i}", writes=[("ma", sk, i)])
                for i in range(4, 6):
                    P.dma("gpsimd", wada[i][:], kcview(wada_d.ap()[:, i * 512:(i + 1) * 512]), f"d_wa{i}", writes=[("wada", i)])

                P.op("scalar", lambda e: e.activation(out=sct[:], in_=ct[:], func=AF.Silu), reads=["ct"], writes=["sct"])
                P.op("scalar", lambda e: e.copy(out=sct_b[:], in_=sct[:]), reads=["sct"], writes=["sct_b"])
                P.op("vector", lambda e: e.memset(ones[:], 1.0), writes=["ones"])
                for kc in range(8):
                    for q in range(2):
                        P.op("vector", lambda e, kc=kc, q=q: e.tensor_scalar(
                            out=rep[:, kc, q, :], in0=ones[:], scalar1=sct[:, kc, q:q + 1], scalar2=None, op0=ALU.mult),
                            reads=["ones", "sct"], writes=[("rep", kc, q)])
                for ch in range(16):
                    piece = wada[ch // 4]
                    col = (ch % 4) * 128
                    mm(ps[:, 0, 2 * ch:2 * ch + 2], [(piece[:, kc, col:col + 128], sct_b[:, kc, :]) for kc in range(8)],
                       reads=[("wada", ch // 4), "sct_b"], writes=[("ps", 0)])
                P.op("vector", lambda e: e.tensor_tensor(out=ss[:], in0=ps[:, 0, 0:32].rearrange("p (c q) -> p c q", q=2),
                                                         in1=bfm[:], op=ALU.add),
                     reads=[("ps", 0), "bfm"], writes=["ss"])
                P.op("vector", lambda e: e.tensor_scalar_add(out=ss[:, 8:16, :], in0=ss[:, 8:16, :], scalar1=1.0),
                     reads=["ss"], writes=["ss"])
                batches = [(sk, b) for sk in ("P", "S") for b in range(SEQS[sk]["N2"] // 4)]

                def load_x(bi):
                    sk, b = batches[bi]
                    xb = bi % 2
                    P.dma("sync", xt[xb][:], kcview(xTp[sk].ap()[:, b * 512:(b + 1) * 512]), f"d_xt{xb}",
                          writes=[("xt", xb)])

                def hconv(bi):
                    sk, b = batches[bi]
                    q = SEQS[sk]["q"]
                    xb = bi % 2
                    for kc in range(8):
                        P.op("scalar", lambda e, xb=xb, kc=kc, q=q: e.activation(
                            out=ht[xb][:, kc, :], in_=xt[xb][:, kc, :], func=AF.Identity,
                            scale=ss[:, 8 + kc, q:q + 1], bias=ss[:, kc, q:q + 1]),
                            reads=[("xt", xb), "ss"], writes=[("ht", xb, kc)])

                def stage_a(bi, j):
                    sk, b_ = batches[bi]
                    s2 = 4 * b_ + j
                    vb = (4 * bi + j) % 2
                    gslot = bi % 2
                    par = (4 * bi + j) % 2
                    for half in range(2):
                        bank = 4 + 2 * par + half
                        mm(ps[:, bank, :], [(ma[sk][:, s2, :], vsb[vb][:, half * 512:(half + 1) * 512])],
                           reads=[("ma", sk, s2 // 16), ("vsb", vb, half)], writes=[("ps", bank)])
                        if half == 0:
                            P.op("scalar", lambda e, gslot=gslot, j=j, half=half, bank=bank: e.copy(
                                out=gsb[gslot][:, j, half * 512:(half + 1) * 512], in_=ps[:, bank, :]),
                                reads=[("ps", bank)], writes=[("gsb", gslot, j, half)])
                        else:
                            P.op("vector", lambda e, gslot=gslot, j=j, half=half, bank=bank: e.tensor_copy(
                                out=gsb[gslot][:, j, half * 512:(half + 1) * 512], in_=ps[:, bank, :]),
                                reads=[("ps", bank)], writes=[("gsb", gslot, j, half)])
                    if j == 3:
                        for ri in range(2):
                            P.dma("gpsimd", Gd[sk].ap()[ri, 4 * b_:4 * b_ + 4].rearrange("a k f -> k a f"),
                                  gsb[gslot][ri * 64:(ri + 1) * 64, :, :], f"d_g{gslot}{ri}",
                                  reads=[("gsb", gslot, jj, h) for jj in range(4) for h in range(2)])

                nbt = len(batches)
                load_x(0)
                load_x(1)
                hconv(0)
                prev = None
                for bi in range(nbt):
                    xb = bi % 2
                    if bi + 1 < nbt:
                        hconv(bi + 1)
                    if bi + 2 < nbt:
                        load_x(bi + 2)
                    for j in range(4):
                        vb = (4 * bi + j) % 2
                        for half in range(2):
                            mm(ps[:, vb * 2 + half, :],
                               [(ht[xb][:, kc, j * 128:(j + 1) * 128], wvf_b[:, kc, half * 512:(half + 1) * 512]) for kc in range(8)],
                               reads=[("ht", xb, kc) for kc in range(8)] + [("wvf", half)], writes=[("ps", vb * 2 + half)])
                            P.op("vector", lambda e, vb=vb, half=half: e.tensor_copy(
                                out=vsb[vb][:, half * 512:(half + 1) * 512], in_=ps[:, vb * 2 + half, :]),
                                reads=[("ps", vb * 2 + half)], writes=[("vsb", vb, half)])
                        if prev is not None:
                            stage_a(*prev)
                        prev = (bi, j)
                stage_a(*prev)
                for q in range(2):
                    for half in range(2):
                        bank = 4 + q * 2 + half
                        mm(ps[:, bank, :], [(rep[:, kc, q, :], wada[4 + half][:, kc, :]) for kc in range(8)],
                           reads=[("rep", kc, q) for kc in range(8)] + [("wada", 4 + half)], writes=[("ps", bank)])
                        P.op("vector", lambda e, q=q, half=half, bank=bank: e.tensor_tensor(
                            out=gate_sb[:, q, half * 512:(half + 1) * 512], in0=ps[:, bank, :],
                            in1=bc0[:, half * 512:(half + 1) * 512], op=ALU.add),
                            reads=[("ps", bank), "bc0"], writes=[("gate_sb", q, half)])
                P.dma("gpsimd", gate_d.ap(), gate_sb[:], "d_gs",
                      reads=[("gate_sb", q, h) for q in range(2) for h in range(2)], writes=["gate_d"])
                P.barrier()
                P.emit()

        with ExitStack() as W:
            win_b = sb(W, "win_b", [128, 8, 5120], BF16)
            F2S = ExitStack()
            cw_b = sb(F2S, "cw_b", [128, 2, 8, 128], BF16)
            with ExitStack() as C2:
                gt = [sb(C2, f"gt{i}", [128, 32, 256], BF16) for i in range(3)]
                psb = [sb(C2, f"psb{i}", [128, 2, 32, 128], BF16) for i in range(2)]
                frs = [sb(C2, f"frs{i}", [128, 4096], BF16) for i in range(2)]
                cdt = sb(C2, "cdt", [128, 2, 128], F32)
                wfm = sb(C2, "wfm", [128, 1024], F32)
                P.dma("sync", cdt[:], cdt_d.ap(), "d_C", track=False)
                evC = P.dma("sync", wfm[:].rearrange("p (g e) -> p g e", g=8), wfm_d.ap().rearrange("g m e -> m g e"), "d_C", track=False)
                P.note(evC, writes=["cdt", "wfm"])
                for cs in range(2):
                    for gh in range(2):
                        bank = 4 + cs * 2 + gh
                        mm(ps[:, bank, :], [(cdt[:, cs, :], wfm[:, gh * 512:(gh + 1) * 512])],
                           reads=["cdt", "wfm"], writes=[("ps", bank)])
                        P.op("scalar", lambda e, cs=cs, gh=gh, bank=bank: e.copy(
                            out=cw_b[:, cs, gh * 4:(gh + 1) * 4, :], in_=ps[:, bank, :].rearrange("p (g e) -> p g e", g=4)),
                            reads=[("ps", bank)], writes=["cw_b"])
                for i in range(10):
                    src = i * 512 if i < 8 else 5120 + (i - 8) * 512
                    P.dma("gpsimd", win_b[:, :, i * 512:(i + 1) * 512], kcview(win_d.ap()[:, src:src + 512]), f"d_wr{i % 4}", writes=["wres"])
                gcount = 0
                gtc = 0
                bankc = 0
                cc = 0
                ev_i = 0
                for sk in ("P", "S"):
                    N2 = SEQS[sk]["N2"]
                    K2 = 2 * N2
                    T = 64 * N2
                    H2 = N2 // 2
                    nslot = 512 // N2
                    Gv = Gd[sk].ap().rearrange("r a k f -> (r a) k f")
                    psb_all = {pb: [("psb", pb, kk) for kk in range(128)] for pb in range(2)}
                    for gp in range(4):
                        pbs = [0, 1]
                        gcount += 1
                        for kb in range(2):
                            gi = gtc % 3
                            gtc += 1
                            P.dma("sync", gt[gi][0:K2], Gv[:, kb * 32:(kb + 1) * 32, gp * 256:(gp + 1) * 256], f"d_gt{gi}",
                                  writes=[("gt", gi)])
                            runs = [[(k, k - 32 * kb, 2 if k == 0 else 0) for k in range(32 * kb, 32 * kb + 32)]]
                            if kb == 0:
                                runs.append([(64, 0, 3)])
                                runs.append([(k, 128 - k, 1) for k in range(97, 128)])
                            else:
                                runs.append([(k, 128 - k - 32, 1) for k in range(65, 97)])
                            chunks = []
                            for run in runs:
                                for c0 in range(0, len(run), 2 * nslot):
                                    chunks.append(run[c0:c0 + 2 * nslot])
                            for gg in range(2):
                                pb = pbs[gg]
                                for chunk in chunks:
                                    bset = (bankc % 3) * 2
                                    bankc += 1

                                    def fnb(e, gi=gi, chunk=chunk, bset=bset, N2=N2, K2=K2, nslot=nslot, sk=sk, gg=gg):
                                        ins = None
                                        for idx, (k1o, loc, mat) in enumerate(chunk):
                                            bb, sl = divmod(idx, nslot)
                                            ins = e.matmul(ps[:, bset + bb, sl * N2:(sl + 1) * N2],
                                                           lhsT=gt[gi][0:K2, loc, gg * 128:(gg + 1) * 128],
                                                           rhs=mb[sk][0:K2, mat, :], start=True, stop=True)
                                        return ins
                                    P.op("tensor", fnb, reads=[("gt", gi), "mb"], writes=[("ps", bset), ("ps", bset + 1)])
                                    n = len(chunk)
                                    k1_0 = chunk[0][0]
                                    keys = [("psb", pb, kk) for kk in range(k1_0, k1_0 + n)]
                                    oap = psb[pb][:, :, 0:H2, k1_0:k1_0 + n]
                                    iap = ps[:, bset:bset + 2, :].rearrange("p b (s r k) -> p r k (b s)", s=nslot, r=2)[:, :, :, 0:n]
                                    if ev_i % 2 == 0:
                                        P.op("scalar", lambda e, oap=oap, iap=iap: e.copy(out=oap, in_=iap),
                                             reads=[("ps", bset), ("ps", bset + 1)], writes=keys)
                                    else:
                                        P.op("vector", lambda e, oap=oap, iap=iap: e.tensor_copy(out=oap, in_=iap),
                                             reads=[("ps", bset), ("ps", bset + 1)], writes=keys)
                                    ev_i += 1
                        for gg in range(2):
                            g = gp * 2 + gg
                            pb = pbs[gg]
                            fb = pb
                            frkeys = []
                            for j in range(T // 512):
                                bank = 6 + cc % 2
                                cc += 1
                                mm(ps[:, bank, :],
                                   [(cw_b[:, cs, g, :], psb[pb][:, cs, 4 * j:4 * j + 4, :].rearrange("p a b -> p (a b)")) for cs in range(2)],
                                   reads=psb_all[pb] + ["cw_b"], writes=[("ps", bank)])
                                key = ("frs", fb, j)
                                frkeys.append(key)
                                if ev_i % 2 == 0:
                                    P.op("scalar", lambda e, fb=fb, j=j, bank=bank: e.copy(out=frs[fb][:, j * 512:(j + 1) * 512], in_=ps[:, bank, :]),
                                         reads=[("ps", bank)], writes=[key])
                                else:
                                    P.op("vector", lambda e, fb=fb, j=j, bank=bank: e.tensor_copy(out=frs[fb][:, j * 512:(j + 1) * 512], in_=ps[:, bank, :]),
                                         reads=[("ps", bank)], writes=[key])
                                ev_i += 1
                            P.dma("scalar", FRd[sk].ap()[g], frs[fb][:, 0:T], f"d_fr{fb}", reads=frkeys)
                P.barrier()
                P.emit()
            F2S.close()
            with ExitStack() as M:
                wout_b = sb(M, "wout_b", [128, 16, 1024], BF16)
                for kh in range(2):
                    for cq in range(2):
                        P.dma("gpsimd", wout_b[:, kh * 8:(kh + 1) * 8, cq * 512:(cq + 1) * 512],
                              kcview(wout_d.ap()[kh * 1024:(kh + 1) * 1024, cq * 512:(cq + 1) * 512]), f"d_wo{kh * 2 + cq}", writes=["wout"])
                xt = sb(M, "m_xt", [128, 8, 512], F32)
                ht = sb(M, "m_ht", [128, 8, 512], BF16)
                yt = sb(M, "m_yt", [128, 16, 512], BF16)
                fr = sb(M, "m_fr", [128, 2, 512], BF16)
                xtok = sb(M, "m_xtok", [128, 1, 1024], F32)
                rbuf = sb(M, "m_rbuf", [128, 2, 1024], F32)
                ybuf = sb(M, "m_ybuf", [128, 2, 1024], F32)
                bc = sb(M, "m_bc", [128, 3, 1024], F32)
                cgsb = sb(M, "m_cgsb", [128, 2, 512], F32)
                sg = sb(M, "m_sg", [128, 2, 512], F32)
                uext = sb(M, "m_uext", [128, 514], F32)
                acc = sb(M, "m_acc", [128, 512], F32)
                y1 = sb(M, "m_y1", [128, 1, 512], F32)
                xth = sb(M, "m_xth", [128, 8, 32], F32)
                hth = sb(M, "m_hth", [128, 8, 32], BF16)
                cgh = sb(M, "m_cgh", [128, 8, 32], F32)
                uh = sb(M, "m_uh", [128, 8, 32], F32)
                hmask = sb(M, "m_hmask", [128, 8, 32], F32)
                stats = sb(M, "m_stats", [128, 12], F32)
                mv = sb(M, "m_mv", [128, 2], F32)
                rstd = sb(M, "m_rstd", [128, 1], F32)
                epst = sb(M, "m_epst", [128, 1], F32)
                nmr = sb(M, "m_nmr", [128, 1], F32)
                P.op("vector", lambda e: e.memset(epst[:], LN_EPS), writes=["epst"])

                P.dma("sync", xth[:], kcview(xTh_d.ap()), "d_M", track=False)
                P.dma("sync", hmask[:], hmask_d.ap(), "d_M", track=False)
                evM = P.dma("sync", bc[:, 1:3, :], bcast_d.ap()[:, 1:3, :], "d_M", track=False)
                P.note(evM, writes=["xth", "hmask", "lngb"])
                for kc in range(8):
                    for q, (c0, c1) in ((0, (0, 16)), (1, (16, 32))):
                        P.op("scalar", lambda e, kc=kc, q=q, c0=c0, c1=c1: e.activation(
                            out=hth[:, kc, c0:c1], in_=xth[:, kc, c0:c1], func=AF.Identity,
                            scale=ss[:, 8 + kc, q:q + 1], bias=ss[:, kc, q:q + 1]),
                            reads=["xth", "ss"], writes=[("hth", kc, q)])
                hth_keys = [("hth", kc, q) for kc in range(8) for q in range(2)]
                for c in range(8):
                    mm(ps[:, 0, c * 32:(c + 1) * 32], [(win_b[:, kc, 1024 + c * 128:1024 + (c + 1) * 128], hth[:, kc, :]) for kc in range(8)],
                       reads=hth_keys + ["wres"], writes=[("ps", 0)])
                    mm(ps[:, 1, c * 32:(c + 1) * 32], [(win_b[:, kc, 2048 + c * 128:2048 + (c + 1) * 128], hth[:, kc, :]) for kc in range(8)],
                       reads=hth_keys + ["wres"], writes=[("ps", 1)])
                P.op("scalar", lambda e: e.copy(out=cgh[:], in_=ps[:, 0, 0:256].rearrange("p (c t) -> p c t", c=8)),
                     reads=[("ps", 0)], writes=["cgh"])
                P.op("vector", lambda e: e.tensor_tensor(out=uh[:], in0=cgh[:], in1=ps[:, 1, 0:256].rearrange("p (c t) -> p c t", c=8), op=ALU.mult),
                     reads=["cgh", ("ps", 1)], writes=["uh"])
                P.op("vector", lambda e: e.tensor_tensor(out=uh[:], in0=uh[:], in1=hmask[:], op=ALU.mult),
                     reads=["uh", "hmask"], writes=["uh"])

                hbase = {"S": 0, "P": 16}
                tiles = [(sk, i) for sk in ("P", "S") for i in range(SEQS[sk]["S"] // 2 // 512)]

                def load_xt(ti):
                    sk, i = tiles[ti]
                    P.dma("sync", xt[:], kcview(xTo[sk].ap()[:, i * 512:(i + 1) * 512]), "d_mxt", writes=["xt"])

                def hconv_m(ti):
                    sk, i = tiles[ti]
                    q = SEQS[sk]["q"]
                    for kc in range(8):
                        P.op("scalar", lambda e, kc=kc, q=q: e.activation(
                            out=ht[:, kc, :], in_=xt[:, kc, :], func=AF.Identity,
                            scale=ss[:, 8 + kc, q:q + 1], bias=ss[:, kc, q:q + 1]),
                            reads=["xt", "ss"], writes=[("ht", kc)])

                ht_keys = [("ht", kc) for kc in range(8)]
                yt_keys = [("yt", kc) for kc in range(16)]
                load_xt(0)
                hconv_m(0)
                for ti, (sk, i) in enumerate(tiles):
                    q = SEQS[sk]["q"]
                    if ti + 1 < len(tiles):
                        load_xt(ti + 1)
                    hl = hbase[sk] + 2 * i
                    for c in range(8):
                        st_ = (c % 2) * 4
                        tb = c % 2
                        for wi, off in enumerate((0, 1024, 2048, 3072)):
                            mm(ps[:, st_ + wi, :],
                               [(win_b[:, kc, off + c * 128:off + (c + 1) * 128], ht[:, kc, :]) for kc in range(8)],
                               reads=ht_keys + ["wres"], writes=[("ps", st_ + wi)])
                        P.op("scalar", lambda e, tb=tb, st_=st_: e.copy(out=cgsb[:, tb, :], in_=ps[:, st_ + 1, :]),
                             reads=[("ps", st_ + 1)], writes=[("cgsb", tb)])
                        P.op("scalar", lambda e, tb=tb, st_=st_: e.activation(out=sg[:, tb, :], in_=ps[:, st_ + 3, :], func=AF.Silu),
                             reads=[("ps", st_ + 3)], writes=[("sg", tb)])
                        P.op("vector", lambda e, c=c, hl=hl: e.tensor_copy(out=uext[:, 0:1], in_=uh[:, c, hl:hl + 1]),
                             reads=["uh"], writes=["uextL"])
                        P.op("vector", lambda e, c=c, hl=hl: e.tensor_copy(out=uext[:, 513:514], in_=uh[:, c, hl + 1:hl + 2]),
                             reads=["uh"], writes=["uextR"])
                        P.op("vector", lambda e, tb=tb, st_=st_: e.tensor_tensor(out=uext[:, 1:513], in0=cgsb[:, tb, :], in1=ps[:, st_ + 2, :], op=ALU.mult),
                             reads=[("cgsb", tb), ("ps", st_ + 2)], writes=["uextM"])
                        P.op("vector", lambda e, c=c: e.tensor_scalar(out=acc[:], in0=uext[:, 1:513], scalar1=convp[:, c, 1:2], scalar2=convp[:, c, 3:4],
                                                                     op0=ALU.mult, op1=ALU.add),
                             reads=["uextM", "convp"], writes=["acc"])
                        P.op("vector", lambda e, c=c: e.scalar_tensor_tensor(out=acc[:], in0=uext[:, 0:512], scalar=convp[:, c, 0:1], in1=acc[:],
                                                                            op0=ALU.mult, op1=ALU.add),
                             reads=["uextM", "uextL", "acc"], writes=["acc"])
                        P.op("vector", lambda e, c=c: e.scalar_tensor_tensor(out=acc[:], in0=uext[:, 2:514], scalar=convp[:, c, 2:3], in1=acc[:],
                                                                            op0=ALU.mult, op1=ALU.add),
                             reads=["uextM", "uextR", "acc"], writes=["acc"])
                        P.op("vector", lambda e, tb=tb, st_=st_: e.tensor_tensor(out=y1[:, 0, :], in0=acc[:], in1=ps[:, st_, :], op=ALU.mult),
                             reads=["acc", ("ps", st_)], writes=["y1"])
                        P.op("gpsimd", lambda e, tb=tb, c=c: e.tensor_tensor(out=yt[:, c, :], in0=y1[:, 0, :], in1=sg[:, tb, :], op=ALU.mult),
                             reads=["y1", ("sg", tb)], writes=[("yt", c)])
                    for c in range(8):
                        bank = c
                        fb = c % 2
                        mm(ps[:, bank, :],
                           [(win_b[:, kc, 4096 + c * 128:4096 + (c + 1) * 128], ht[:, kc, :]) for kc in range(8)],
                           reads=ht_keys + ["wres"], writes=[("ps", bank)])
                        P.dma("sync", fr[:, fb, :], FRd[sk].ap()[c, :, i * 512:(i + 1) * 512], f"d_frl{fb}", writes=[("fr", fb)])
                        P.op("scalar", lambda e, fb=fb, bank=bank: e.activation(out=cgsb[:, fb, :], in_=ps[:, bank, :], func=AF.Silu),
                             reads=[("ps", bank)], writes=[("cgsb", fb)])
                        P.op("gpsimd", lambda e, fb=fb, c=c: e.tensor_tensor(out=yt[:, 8 + c, :], in0=cgsb[:, fb, :], in1=fr[:, fb, :], op=ALU.mult),
                             reads=[("cgsb", fb), ("fr", fb)], writes=[("yt", 8 + c)])
                    if ti + 1 < len(tiles):
                        hconv_m(ti + 1)
                    if i == 0:
                        P.dma("sync", bc[:, 0, :], gate_d.ap()[:, q, :], "d_gate", reads=["gate_d"], writes=["gate"])
                    def o_mm(sub):
                        ob = (sub % 2) * 4
                        for half in range(2):
                            steps = []
                            for kc in range(16):
                                steps.append((lambda e, kc=kc, sub=sub, half=half, ob=ob: e.matmul(
                                    ps[:, ob + half, :], lhsT=yt[:, kc, sub * 128:(sub + 1) * 128],
                                    rhs=wout_b[:, kc, half * 512:(half + 1) * 512], start=(kc == 0), stop=(kc == 15)),
                                    [("yt", kc), "wout"]))
                            P.op_steps("tensor", steps, writes=[("ps", ob + half)])

                    def o_t(sub):
                        ob = (sub % 2) * 4
                        rb = sub % 2
                        P.op("vector", lambda e, ob=ob, rb=rb: e.tensor_tensor(out=rbuf[:, rb, :].rearrange("p (a b) -> p a b", a=2), in0=ps[:, ob:ob + 2, :],
                                                                             in1=bc[:, 0, :].rearrange("p (a b) -> p a b", a=2), op=ALU.mult),
                             reads=[("ps", ob), ("ps", ob + 1), "gate"], writes=[("rbuf", rb)])

                    def o_chain(sub):
                        rb = sub % 2
                        r0 = i * 512 + sub * 128
                        P.dma("sync", xtok[:, 0, :], xo[sk].ap()[r0:r0 + 128, :], "d_xtok", writes=["xtok"])
                        P.op("vector", lambda e, rb=rb: e.scalar_tensor_tensor(out=rbuf[:, rb, :], in0=xtok[:, 0, :], scalar=ALPHA, in1=rbuf[:, rb, :],
                                                                              op0=ALU.mult, op1=ALU.add),
                             reads=["xtok", ("rbuf", rb)], writes=[("rbuf", rb)])
                        for hh in range(2):
                            P.op("vector", lambda e, hh=hh, rb=rb: e.bn_stats(out=stats[:, hh * 6:(hh + 1) * 6], in_=rbuf[:, rb, hh * 512:(hh + 1) * 512]),
                                 reads=[("rbuf", rb)], writes=[("stats", hh)])
                        P.op("vector", lambda e: e.bn_aggr(out=mv[:], in_=stats[:]), reads=[("stats", 0), ("stats", 1)], writes=["mv"])
                        P.op("scalar", lambda e: e.activation(out=rstd[:], in_=mv[:, 1:2], func=AF.Sqrt, bias=epst[:, 0:1], scale=1.0),
                             reads=["mv", "epst"], writes=["rstd"])
                        P.op("vector", lambda e: e.reciprocal(out=rstd[:], in_=rstd[:]),
                             reads=["rstd"], writes=["rstd"])
                        P.op("vector", lambda e: e.tensor_scalar(out=nmr[:], in0=mv[:, 0:1], scalar1=-1.0, scalar2=rstd[:, 0:1], op0=ALU.mult, op1=ALU.mult),
                             reads=["mv", "rstd"], writes=["nmr"])
                        P.op("scalar", lambda e, rb=rb: e.activation(out=ybuf[:, rb, :], in_=rbuf[:, rb, :], func=AF.Identity,
                                                                    scale=rstd[:, 0:1], bias=nmr[:, 0:1]),
                             reads=[("rbuf", rb), "rstd", "nmr"], writes=[("ybuf", rb)])
                        P.op("gpsimd", lambda e, rb=rb: e.tensor_tensor(out=ybuf[:, rb, :], in0=ybuf[:, rb, :], in1=bc[:, 1, :], op=ALU.mult),
                             reads=[("ybuf", rb), "lngb"], writes=[("ybuf", rb)])
                        P.op("gpsimd", lambda e, rb=rb: e.tensor_tensor(out=ybuf[:, rb, :], in0=ybuf[:, rb, :], in1=bc[:, 2, :], op=ALU.add),
                             reads=[("ybuf", rb), "lngb"], writes=[("ybuf", rb)])
                        P.dma("gpsimd", y_d[sk].ap()[r0:r0 + 128, :], ybuf[:, rb, :], f"d_y{rb}", reads=[("ybuf", rb)])

                    for sub in range(4):
                        o_mm(sub)
                        o_t(sub)
                        o_chain(sub)
                P.barrier()
                P.emit()
    return nc


def _tables(S, r):
    N2 = S // 128
    s1 = np.arange(128)
    s2 = np.arange(N2)
    k1p = np.arange(64)
    s = (N2 * s1[None, :, None] + s2[:, None, None])
    ang = 2 * np.pi * ((k1p[None, None, :] * s) % S) / S
    re = np.cos(ang)
    im = -np.sin(ang)
    im[:, :, 0] = ((-1.0) ** s1)[None, :]
    MA = np.concatenate([re, im], axis=2) / np.sqrt(128.0)
    H2 = N2 // 2
    k2 = r * H2 + np.arange(H2)
    th = 2 * np.pi * ((k2[None, :] * s2[:, None]) % N2) / N2
    c, sn = np.cos(th), np.sin(th)
    Md = np.zeros((2, N2, 2, H2))
    Md[0, :, 0] = c
    Md[0, :, 1] = -sn
    Md[1, :, 0] = sn
    Md[1, :, 1] = c
    th1 = 2 * np.pi * (((k2[None, :] + 1) * s2[:, None]) % N2) / N2
    c1, s1_ = np.cos(th1), np.sin(th1)
    Mm = np.zeros((2, N2, 2, H2))
    Mm[0, :, 0] = c1
    Mm[0, :, 1] = -s1_
    Mm[1, :, 0] = -s1_
    Mm[1, :, 1] = -c1
    Md0 = Md.copy()
    Md0[1] = 0
    th64 = th + np.pi * s2[:, None] / N2
    M64 = np.zeros((2, N2, 2, H2))
    M64[1, :, 0] = np.cos(th64)
    M64[1, :, 1] = -np.sin(th64)
    MB4 = np.stack([Md, Mm, Md0, M64], axis=0).reshape(4, 2 * N2, N2) / np.sqrt(float(N2))
    return (np.ascontiguousarray(MA).astype(ml_dtypes.bfloat16),
            np.ascontiguousarray(MB4.transpose(1, 0, 2)).astype(ml_dtypes.bfloat16))


_NC_CACHE = {}


def kernel(x_prompt, x_sample, c_prompt, c_sample, w_ada, b_ada, w_in, conv_w, conv_b,
           w_fmix, w_out, ln_g, ln_b):
    f32 = np.float32
    x_prompt = np.asarray(x_prompt, f32)
    x_sample = np.asarray(x_sample, f32)
    if "nc" not in _NC_CACHE:
        _NC_CACHE["nc"] = build_nc()
    nc = _NC_CACHE["nc"]

    w_ada2 = np.ascontiguousarray(np.asarray(w_ada, f32)[0])
    b_ada1 = np.asarray(b_ada, f32)[0]
    w_in2 = np.ascontiguousarray(np.asarray(w_in, f32)[0])
    w_out2 = np.ascontiguousarray(np.asarray(w_out, f32)[0])
    w_fm = np.ascontiguousarray(np.asarray(w_fmix, f32)[0])
    cw = np.asarray(conv_w, f32)[0]
    cb = np.asarray(conv_b, f32)[0]
    g1 = np.asarray(ln_g, f32)[0]
    b1 = np.asarray(ln_b, f32)[0]

    b_fm = np.ascontiguousarray(np.repeat(b_ada1[:2048].reshape(16, 128).T[:, :, None], 2, axis=2)).astype(f32)
    bcast = np.ascontiguousarray(np.broadcast_to(np.stack([b_ada1[2048:], g1, b1])[None], (128, 3, 1024))).astype(f32)
    convp = np.ascontiguousarray(np.concatenate([cw, cb[None]], axis=0).reshape(4, 8, 128).transpose(2, 1, 0)).astype(f32)
    d = np.arange(128)
    ang = 2 * np.pi * ((d[:, None] * d[None, :]) % 128) / 128
    cdt = np.ascontiguousarray(np.stack([np.cos(ang), np.sin(ang)], axis=1) / np.sqrt(128.0)).astype(f32)

    tabs = {}
    for r in range(2):
        tabs[("S", r)] = _tables(8192, r)
        tabs[("P", r)] = _tables(4096, r)

    in_maps = []
    for c in range(8):
        j, r = c % 4, c // 4
        m = {}
        xs = x_sample[j]
        xp = x_prompt[j]
        m["xTp_S"] = np.ascontiguousarray(xs.reshape(128, 64, 1024).transpose(2, 1, 0).reshape(1024, 8192))
        m["xTp_P"] = np.ascontiguousarray(xp.reshape(128, 32, 1024).transpose(2, 1, 0).reshape(1024, 4096))
        xso = xs[r * 4096:(r + 1) * 4096]
        xpo = xp[r * 2048:(r + 1) * 2048]
        m["xTo_S"] = np.ascontiguousarray(xso.T)
        m["xTo_P"] = np.ascontiguousarray(xpo.T)
        m["xo_S"] = np.ascontiguousarray(xso)
        m["xo_P"] = np.ascontiguousarray(xpo)
        xh = np.zeros((32, 1024), f32)
        hm = np.zeros((32,), f32)
        for (base, xfull, t0, ntile) in ((0, xs, r * 4096, 8), (16, xp, r * 2048, 4)):
            Sfull = xfull.shape[0]
            for i in range(ntile):
                for side, tok in ((0, t0 + 512 * i - 1), (1, t0 + 512 * i + 512)):
                    if 0 <= tok < Sfull:
                        xh[base + 2 * i + side] = xfull[tok]
                        hm[base + 2 * i + side] = 1.0
        m["xTh"] = np.ascontiguousarray(xh.T)
        m["hmask"] = np.ascontiguousarray(np.broadcast_to(hm[None, None, :], (128, 8, 32))).astype(f32)
        cc = np.stack([np.asarray(c_sample, f32)[j], np.asarray(c_prompt, f32)[j]], axis=1)
        m["cT"] = np.ascontiguousarray(cc.reshape(8, 128, 2).transpose(1, 0, 2))
        m["w_ada"] = w_ada2
        m["b_fm"] = b_fm
        m["bcast"] = bcast
        m["w_in"] = w_in2
        m["w_out"] = w_out2
        m["w_fmix"] = w_fm
        m["convp"] = convp
        m["cdt"] = cdt
        m["MA_S"], m["MB_S"] = tabs[("S", r)]
        m["MA_P"], m["MB_P"] = tabs[("P", r)]
        in_maps.append(m)

    res = run_bass_kernel_spmd(nc, in_maps, core_ids=list(range(8)))
    y_prompt = np.empty((4, 4096, 1024), f32)
    y_sample = np.empty((4, 8192, 1024), f32)
    for c in range(8):
        j, r = c % 4, c // 4
        y_sample[j, r * 4096:(r + 1) * 4096] = res.results[c]["y_S"]
        y_prompt[j, r * 2048:(r + 1) * 2048] = res.results[c]["y_P"]
    return (y_prompt, y_sample)
```
